# Optimizing a Trainium2 kernel written in Bass

```python
import jax
import jax.numpy as jnp
from jax import lax
import numpy as np

D_MODEL = 1024
BATCH = 4
SEQ = 4096
DEPTH = 2

CTX_LEN = 256
GRID_W = 64
ROPE_BASE = 10000.0
EPS = 1e-6
HALF = 0.5
N_MOD = 9
D_FF = 2816

A_HEADS = 4
A_DK = 32
A_DV = 64
A_CHUNK = 64
B_HEADS = 6
B_Q_RANK = 256
B_KV_RANK = 128
B_NOPE = 64
B_ROPE = 32
B_DV = 64
B_DQK = B_NOPE + B_ROPE
B_SCALE = B_DQK ** -0.5
DENSE_QBLOCK = 128
C_HEADS = 6
C_KV_HEADS = 2
C_GROUP = C_HEADS // C_KV_HEADS
C_DH = 64
C_SCALE = C_DH ** -0.5
WINDOW = 128
C_BLOCK = 128

IN_SIZES = (A_HEADS * A_DK, A_HEADS * A_DK, A_HEADS * A_DV, A_HEADS * A_DV, 4 * A_HEADS,
            B_Q_RANK, B_KV_RANK, B_ROPE,
            C_HEADS * C_DH, C_KV_HEADS * C_DH, C_KV_HEADS * C_DH)
D_IN = sum(IN_SIZES)
MIX_WIDTH = A_HEADS * A_DV + B_HEADS * B_DV + C_HEADS * C_DH

kernel_name = "hybrid_mlstm_mla_swa_macaron_dit"


def rms_norm(x, g):
    xf = x.astype(jnp.float32)
    y = xf * lax.rsqrt(jnp.mean(xf * xf, axis=-1, keepdims=True) + EPS)
    return (y * g.astype(jnp.float32)).astype(x.dtype)


def modulate(h, g, shift, scale):
    return rms_norm(h, g) * (1 + scale) + shift


def swiglu(h, wi, wo):
    gt, up = jnp.split(h @ wi, 2, axis=-1)
    return (jax.nn.silu(gt) * up) @ wo


def rope_1d(x, pos):
    half = x.shape[-1] // 2
    inv = ROPE_BASE ** (-jnp.arange(half, dtype=jnp.float32) / half)
    ang = pos.astype(jnp.float32)[:, None] * inv
    cos = jnp.cos(ang)[:, None, :]
    sin = jnp.sin(ang)[:, None, :]
    xf = x.astype(jnp.float32)
    x1, x2 = xf[..., :half], xf[..., half:]
    return jnp.concatenate([x1 * cos - x2 * sin, x1 * sin + x2 * cos], axis=-1).astype(x.dtype)


def rope_2d(x, row, col):
    r = x.shape[-1] // 2
    return jnp.concatenate([rope_1d(x[..., :r], row), rope_1d(x[..., r:], col)], axis=-1)


def split_cols(p):
    return jnp.split(p, [int(i) for i in np.cumsum(IN_SIZES)[:-1]], axis=-1)


def mlstm_chunked(q, k, v, ig, lf, state):
    B, H, T, DK = q.shape
    L = A_CHUNK
    NC = T // L

    def to_chunks(a):
        return jnp.moveaxis(a.reshape((B, H, NC, L) + a.shape[3:]), 2, 0)

    mask = jnp.tril(jnp.ones((L, L), dtype=bool))

    def step(carry, inp):
        C, n, m = carry
        qq, kk, vv, ii, ff = inp
        b = jnp.cumsum(ff, axis=-1)
        dlog = jnp.where(mask, b[..., :, None] - b[..., None, :] + ii[..., None, :], -jnp.inf)
        inter = b + m[..., None]
        m_t = jnp.maximum(inter, jnp.max(dlog, axis=-1))
        w = jnp.exp(dlog - m_t[..., None])
        a_inter = jnp.exp(inter - m_t)
        s = jnp.einsum('bhtd,bhsd->bhts', qq, kk) * w
        num = a_inter[..., None] * jnp.einsum('bhtd,bhde->bhte', qq, C) + jnp.einsum('bhts,bhse->bhte', s, vv)
        den = a_inter * jnp.einsum('bhtd,bhd->bht', qq, n) + jnp.sum(s, axis=-1)
        h = num / jnp.maximum(jnp.abs(den), jnp.exp(-m_t))[..., None]
        b_last = b[..., -1]
        g = b_last[..., None] - b + ii
        m_new = jnp.maximum(b_last + m, jnp.max(g, axis=-1))
        decay = jnp.exp(b_last + m - m_new)
        wg = jnp.exp(g - m_new[..., None])
        C_new = decay[..., None, None] * C + jnp.einsum('bhs,bhsd,bhse->bhde', wg, kk, vv)
        n_new = decay[..., None] * n + jnp.einsum('bhs,bhsd->bhd', wg, kk)
        return (C_new, n_new, m_new), h

    state, hs = lax.scan(step, state, tuple(to_chunks(a) for a in (q, k, v, ig, lf)))
    h = jnp.moveaxis(hs, 0, 2).reshape(B, H, T, v.shape[-1])
    return h, state


def mlstm_branch(parts_l, parts_c, gate_b, out_norm, need_ctx):
    def prep(q, k, v, g):
        B, T, _ = q.shape

        def hd(a, d):
            return a.reshape(B, T, A_HEADS, d).transpose(0, 2, 1, 3).astype(jnp.float32)

        g = (g.astype(jnp.float32) + gate_b.astype(jnp.float32)).reshape(B, T, 4, A_HEADS).transpose(2, 0, 3, 1)
        return (hd(q, A_DK) * A_DK ** -0.5, hd(k, A_DK), hd(v, A_DV),
                g[0], jax.nn.log_sigmoid(g[1]), g[2], jax.nn.log_sigmoid(g[3]))

    def rev(a):
        return jnp.flip(a, axis=2)

    def both(q, k, v, ig_f, lf_f, ig_b, lf_b, st_f, st_b):
        h_f, st_f = mlstm_chunked(q, k, v, ig_f, lf_f, st_f)
        h_b, st_b = mlstm_chunked(rev(q), rev(k), rev(v), rev(ig_b), rev(lf_b), st_b)
        return h_f + rev(h_b), st_f, st_b

    def finish(h, o):
        B, _, T, _ = h.shape
        h = rms_norm(h.transpose(0, 2, 1, 3), out_norm.reshape(A_HEADS, A_DV))
        return (jax.nn.sigmoid(o.astype(jnp.float32)) * h.reshape(B, T, A_HEADS * A_DV)).astype(o.dtype)

    in_c = prep(parts_c[0], parts_c[1], parts_c[2], parts_c[4])
    in_l = prep(parts_l[0], parts_l[1], parts_l[2], parts_l[4])
    B = in_c[0].shape[0]
    zero = (jnp.zeros((B, A_HEADS, A_DK, A_DV), jnp.float32),
            jnp.zeros((B, A_HEADS, A_DK), jnp.float32),
            jnp.zeros((B, A_HEADS), jnp.float32))
    h_c, st_f, st_b = both(*in_c, zero, zero)
    h_l, _, _ = both(*in_l, st_f, st_b)
    y_l = finish(h_l, parts_l[3])
    y_c = finish(h_c, parts_c[3]) if need_ctx else None
    return y_l, y_c


def dense_attend(q, k, v, scale):
    s = jnp.einsum('bhqd,bhkd->bhqk', q, k).astype(jnp.float32) * scale
    p = jax.nn.softmax(s, axis=-1)
    return jnp.einsum('bhqk,bhkd->bhqd', p.astype(v.dtype), v)


def mla_branch(parts_l, parts_c, cq_norm, ckv_norm, w_uq, w_ukv, q_norm, k_norm, row, col, need_ctx):
    def heads(cq, ckv, kr, rotate):
        B, T, _ = cq.shape
        q = (rms_norm(cq, cq_norm) @ w_uq).reshape(B, T, B_HEADS, B_DQK)
        kv = (rms_norm(ckv, ckv_norm) @ w_ukv).reshape(B, T, B_HEADS, B_NOPE + B_DV)
        k = jnp.concatenate([kv[..., :B_NOPE], jnp.broadcast_to(kr[:, :, None, :], (B, T, B_HEADS, B_ROPE))], axis=-1)
        v = kv[..., B_NOPE:]
        q = rms_norm(q, q_norm)
        k = rms_norm(k, k_norm)
        if rotate:
            q = jnp.concatenate([q[..., :B_NOPE], rope_2d(q[..., B_NOPE:], row, col)], axis=-1)
            k = jnp.concatenate([k[..., :B_NOPE], rope_2d(k[..., B_NOPE:], row, col)], axis=-1)
        return q.transpose(0, 2, 1, 3), k.transpose(0, 2, 1, 3), v.transpose(0, 2, 1, 3)

    q_l, k_l, v_l = heads(parts_l[0], parts_l[1], parts_l[2], True)
    q_c, k_c, v_c = heads(parts_c[0], parts_c[1], parts_c[2], False)
    B, H, T, _ = q_l.shape
    k_all = jnp.concatenate([k_c, k_l], axis=2)
    v_all = jnp.concatenate([v_c, v_l], axis=2)
    nq = T // DENSE_QBLOCK
    qb = q_l.reshape(B, H, nq, DENSE_QBLOCK, B_DQK).transpose(2, 0, 1, 3, 4)
    ob = lax.map(lambda qq: dense_attend(qq, k_all, v_all, B_SCALE), qb)
    y_l = ob.transpose(1, 0, 3, 2, 4).reshape(B, T, H * B_DV)
    y_c = None
    if need_ctx:
        y_c = dense_attend(q_c, k_c, v_c, B_SCALE).transpose(0, 2, 1, 3).reshape(B, q_c.shape[2], H * B_DV)
    return y_l, y_c


def gqa_branch(parts_l, parts_c, q_norm, k_norm, sink, row, col, need_ctx):
    def heads(q, k, v, rotate):
        B, T, _ = q.shape
        q = rms_norm(q.reshape(B, T, C_HEADS, C_DH), q_norm)
        k = rms_norm(k.reshape(B, T, C_KV_HEADS, C_DH), k_norm)
        v = v.reshape(B, T, C_KV_HEADS, C_DH)
        if rotate:
            q = rope_2d(q, row, col)
            k = rope_2d(k, row, col)
        q = q.reshape(B, T, C_KV_HEADS, C_GROUP, C_DH).transpose(0, 2, 3, 1, 4)
        return q, k.transpose(0, 2, 1, 3), v.transpose(0, 2, 1, 3)

    q_l, k_l, v_l = heads(parts_l[0], parts_l[1], parts_l[2], True)
    q_c, k_c, v_c = heads(parts_c[0], parts_c[1], parts_c[2], False)
    B, KVH, G, T, dh = q_l.shape
    n_ctx = k_c.shape[2]
    nb = T // C_BLOCK
    sink_h = sink.astype(jnp.float32).reshape(C_KV_HEADS, C_GROUP)

    def band(a):
        ap = jnp.pad(a, ((0, 0), (0, 0), (C_BLOCK, C_BLOCK), (0, 0))).reshape(B, KVH, nb + 2, C_BLOCK, dh)
        return jnp.concatenate([ap[:, :, :-2], ap[:, :, 1:-1], ap[:, :, 2:]], axis=3)

    kb, vb = band(k_l), band(v_l)
    qb = q_l.reshape(B, KVH, G, nb, C_BLOCK, dh)
    s_band = jnp.einsum('bkgnqd,bknsd->bkgnqs', qb, kb).astype(jnp.float32) * C_SCALE
    blk = jnp.arange(nb)[:, None, None] * C_BLOCK
    qpos = blk + jnp.arange(C_BLOCK)[None, :, None]
    kpos = blk - C_BLOCK + jnp.arange(3 * C_BLOCK)[None, None, :]
    valid = (jnp.abs(qpos - kpos) <= WINDOW) & (kpos >= 0) & (kpos < T)
    s_band = jnp.where(valid, s_band, -jnp.inf)
    s_ctx = jnp.einsum('bkgnqd,bkcd->bkgnqc', qb, k_c).astype(jnp.float32) * C_SCALE
    s_sink = jnp.broadcast_to(sink_h[None, :, :, None, None, None], s_ctx.shape[:-1] + (1,))
    p = jax.nn.softmax(jnp.concatenate([s_sink, s_ctx, s_band], axis=-1), axis=-1)
    o = (jnp.einsum('bkgnqc,bkcd->bkgnqd', p[..., 1:1 + n_ctx].astype(v_c.dtype), v_c)
         + jnp.einsum('bkgnqs,bknsd->bkgnqd', p[..., 1 + n_ctx:].astype(vb.dtype), vb))
    y_l = o.reshape(B, KVH, G, T, dh).transpose(0, 3, 1, 2, 4).reshape(B, T, C_HEADS * C_DH)
    y_c = None
    if need_ctx:
        s = jnp.einsum('bkgtd,bkcd->bkgtc', q_c, k_c).astype(jnp.float32) * C_SCALE
        s0 = jnp.broadcast_to(sink_h[None, :, :, None, None], s.shape[:-1] + (1,))
        pc = jax.nn.softmax(jnp.concatenate([s0, s], axis=-1), axis=-1)[..., 1:]
        oc = jnp.einsum('bkgtc,bkcd->bkgtd', pc.astype(v_c.dtype), v_c)
        y_c = oc.transpose(0, 3, 1, 2, 4).reshape(B, n_ctx, C_HEADS * C_DH)
    return y_l, y_c


def token_mix(h_l, h_c, w_in, mlstm_gate_b, mlstm_out_norm, mla_cq_norm, mla_ckv_norm, mla_w_uq, mla_w_ukv,
              mla_q_norm, mla_k_norm, gqa_q_norm, gqa_k_norm, gqa_sink, row, col, need_ctx):
    p_l = split_cols(h_l @ w_in)
    p_c = split_cols(h_c @ w_in)
    ya_l, ya_c = mlstm_branch(p_l[0:5], p_c[0:5], mlstm_gate_b, mlstm_out_norm, need_ctx)
    yb_l, yb_c = mla_branch(p_l[5:8], p_c[5:8], mla_cq_norm, mla_ckv_norm, mla_w_uq, mla_w_ukv,
                            mla_q_norm, mla_k_norm, row, col, need_ctx)
    yc_l, yc_c = gqa_branch(p_l[8:11], p_c[8:11], gqa_q_norm, gqa_k_norm, gqa_sink, row, col, need_ctx)
    y_l = jnp.concatenate([ya_l, yb_l, yc_l], axis=-1)
    y_c = jnp.concatenate([ya_c, yb_c, yc_c], axis=-1) if need_ctx else None
    return y_l, y_c


def setup_inputs(seed: int = 0) -> dict:
    key = jax.random.key(seed)
    ks = iter(jax.random.split(key, 32))
    f32 = jnp.float32

    def nrm(shape, fan_in, scale=1.0):
        return jax.random.normal(next(ks), shape, f32) * (scale * fan_in ** -0.5)

    def gain(shape):
        return 1.0 + 0.05 * jax.random.normal(next(ks), shape, f32)

    fbias = jnp.linspace(3.0, 6.0, A_HEADS, dtype=f32)
    zb = jnp.zeros((A_HEADS,), f32)
    gate_base = jnp.concatenate([zb, fbias, zb, fbias])
    return {
        'x': jax.random.normal(next(ks), (BATCH, SEQ, D_MODEL), f32),
        'c': jax.random.normal(next(ks), (BATCH, D_MODEL), f32),
        'ctx': jax.random.normal(next(ks), (BATCH, CTX_LEN, D_MODEL), f32),
        'c_ctx': jax.random.normal(next(ks), (D_MODEL,), f32),
        'ada_w': nrm((DEPTH, D_MODEL, N_MOD * D_MODEL), D_MODEL, 0.5),
        'ada_b': 0.02 * jax.random.normal(next(ks), (DEPTH, N_MOD * D_MODEL), f32),
        'norm_g': gain((DEPTH, 3, D_MODEL)),
        'ffn1_wi': nrm((DEPTH, D_MODEL, 2 * D_FF), D_MODEL),
        'ffn1_wo': nrm((DEPTH, D_FF, D_MODEL), D_FF),
        'ffn2_wi': nrm((DEPTH, D_MODEL, 2 * D_FF), D_MODEL),
        'ffn2_wo': nrm((DEPTH, D_FF, D_MODEL), D_FF),
        'w_in': nrm((DEPTH, D_MODEL, D_IN), D_MODEL),
        'w_out': nrm((DEPTH, MIX_WIDTH, D_MODEL), MIX_WIDTH),
        'mlstm_gate_b': gate_base[None, :] + 0.1 * jax.random.normal(next(ks), (DEPTH, 4 * A_HEADS), f32),
        'mlstm_out_norm': gain((DEPTH, A_HEADS * A_DV)),
        'mla_cq_norm': gain((DEPTH, B_Q_RANK)),
        'mla_ckv_norm': gain((DEPTH, B_KV_RANK)),
        'mla_w_uq': nrm((DEPTH, B_Q_RANK, B_HEADS * B_DQK), B_Q_RANK),
        'mla_w_ukv': nrm((DEPTH, B_KV_RANK, B_HEADS * (B_NOPE + B_DV)), B_KV_RANK),
        'mla_q_norm': gain((DEPTH, B_DQK)),
        'mla_k_norm': gain((DEPTH, B_DQK)),
        'gqa_q_norm': gain((DEPTH, C_DH)),
        'gqa_k_norm': gain((DEPTH, C_DH)),
        'gqa_sink': 0.5 * jax.random.normal(next(ks), (DEPTH, C_HEADS), f32),
    }


def reference(x, c, ctx, c_ctx, ada_w, ada_b, norm_g, ffn1_wi, ffn1_wo, ffn2_wi, ffn2_wo, w_in, w_out,
              mlstm_gate_b, mlstm_out_norm, mla_cq_norm, mla_ckv_norm, mla_w_uq, mla_w_ukv, mla_q_norm, mla_k_norm,
              gqa_q_norm, gqa_k_norm, gqa_sink):
    B, T, D = x.shape
    ROWS = T // GRID_W
    row = jnp.repeat(jnp.arange(ROWS, dtype=jnp.int32), GRID_W)
    col = jnp.arange(ROWS * GRID_W, dtype=jnp.int32) % GRID_W
    xc = ctx
    for l in range(DEPTH):
        need_ctx = l < DEPTH - 1
        mod_l = (jax.nn.silu(c) @ ada_w[l] + ada_b[l]).reshape(B, N_MOD, D).transpose(1, 0, 2)[:, :, None, :]
        mod_c = (jax.nn.silu(c_ctx) @ ada_w[l] + ada_b[l]).reshape(N_MOD, 1, 1, D)
        x = x + HALF * mod_l[2] * swiglu(modulate(x, norm_g[l, 0], mod_l[0], mod_l[1]), ffn1_wi[l], ffn1_wo[l])
        xc = xc + HALF * mod_c[2] * swiglu(modulate(xc, norm_g[l, 0], mod_c[0], mod_c[1]), ffn1_wi[l], ffn1_wo[l])
        y_l, y_c = token_mix(modulate(x, norm_g[l, 1], mod_l[3], mod_l[4]),
                             modulate(xc, norm_g[l, 1], mod_c[3], mod_c[4]),
                             w_in[l], mlstm_gate_b[l], mlstm_out_norm[l], mla_cq_norm[l], mla_ckv_norm[l],
                             mla_w_uq[l], mla_w_ukv[l], mla_q_norm[l], mla_k_norm[l],
                             gqa_q_norm[l], gqa_k_norm[l], gqa_sink[l], row, col, need_ctx)
        x = x + mod_l[5] * (y_l @ w_out[l])
        x = x + HALF * mod_l[8] * swiglu(modulate(x, norm_g[l, 2], mod_l[6], mod_l[7]), ffn2_wi[l], ffn2_wo[l])
        if need_ctx:
            xc = xc + mod_c[5] * (y_c @ w_out[l])
            xc = xc + HALF * mod_c[8] * swiglu(modulate(xc, norm_g[l, 2], mod_c[6], mod_c[7]), ffn2_wi[l], ffn2_wo[l])
    return x
```

```python
import concourse.bass as bass
import concourse.mybir as mybir

F32 = mybir.dt.float32
BF16 = mybir.dt.bfloat16
AF = mybir.ActivationFunctionType
ALU = mybir.AluOpType
AX = mybir.AxisListType

ENGS = ("pe", "act", "dve", "pool", "sp")
N_DSEM = 24
DSLOTS = {"sp": (0, 14), "pool": (14, 8), "act": (22, 2)}


class Op:
    __slots__ = ("eng", "fn", "reads", "writes", "dma", "idx", "waits", "signal", "count", "dslot", "dval", "extra")

    def __init__(self, eng, fn, reads, writes, dma):
        self.eng = eng
        self.fn = fn
        self.reads = reads
        self.writes = writes
        self.dma = dma
        self.waits = []
        self.signal = False
        self.count = 0
        self.extra = None


class Prog:
    def __init__(self, nc, same_engine_sync=True):
        self.nc = nc
        self.ops = []
        self.same_engine_sync = same_engine_sync
        self._cap = None

    def op(self, eng, fn, reads=(), writes=(), dma=False):
        o = Op(eng, fn, tuple(reads), tuple(writes), dma)
        if self._cap is not None:
            self._cap[-1].append(o)
            return o
        o.idx = len(self.ops)
        self.ops.append(o)
        return o

    def begin_capture(self):
        self._cap = [[]]

    def mark(self):
        self._cap.append([])

    def end_capture(self, head=1, group=1):
        lists = self._cap
        self._cap = None

        def ins(o):
            o.idx = len(self.ops)
            self.ops.append(o)
        for l in lists[:head]:
            for o in l:
                ins(o)
        rest = [l for l in lists[head:] if l]
        pos = [0] * len(rest)
        active = True
        while active:
            active = False
            for k, l in enumerate(rest):
                for _ in range(group):
                    if pos[k] < len(l):
                        ins(l[pos[k]])
                        pos[k] += 1
                        active = True

    def barrier(self):
        for i, e in enumerate(ENGS):
            o = self.op(e, None)
            o.extra = "last" if i == len(ENGS) - 1 else "bar"

    def dma(self, eng, out, in_, reads, writes, **kw):
        return self.op(eng, lambda e: e.dma_start(out=out, in_=in_, **kw), reads, writes, dma=True)

    def emit(self):
        nc = self.nc
        ops = self.ops
        last_w = {}
        readers = {}
        n_dma = 0
        dma_ops = []
        last_eng = {}
        dma_since = []
        dma_q = {}
        for o in ops:
            deps = set()
            for r in o.reads:
                j = last_w.get(r)
                if j is not None:
                    deps.add(j)
            for w in o.writes:
                j = last_w.get(w)
                if j is not None:
                    deps.add(j)
                rd = readers.get(w)
                if rd:
                    for j in rd.values():
                        deps.add(j)
            if o.dma:
                base, ns = DSLOTS[o.eng]
                lst = dma_q.setdefault(o.eng, [])
                nd = len(lst)
                o.dslot = base + nd % ns
                o.dval = 16 * (nd // ns + 1)
                if nd >= ns:
                    deps.add(lst[nd - ns].idx)
                lst.append(o)
                n_dma += 1
            deps.discard(o.idx)
            o.waits = sorted(deps)
            if o.extra:
                for e2, j in last_eng.items():
                    deps.add(j)
                for j in dma_since:
                    deps.add(j)
                deps.discard(o.idx)
                o.waits = sorted(deps)
                if o.extra == "last":
                    dma_since = []
            if o.fn is not None:
                for w in o.writes:
                    last_w[w] = o.idx
                    readers[w] = {}
                rk = ("dma", o.idx) if o.dma else o.eng
                for r in o.reads:
                    if r not in o.writes:
                        readers.setdefault(r, {})[rk] = o.idx
            if o.fn is not None:
                if o.dma:
                    dma_since.append(o.idx)
                else:
                    last_eng[o.eng] = o.idx
        for o in ops:
            keep = []
            for j in o.waits:
                p = ops[j]
                if p.dma:
                    keep.append(j)
                    continue
                if p.eng == o.eng and not o.dma:
                    if p.eng == "pe" or not self.same_engine_sync:
                        continue
                p.signal = True
                keep.append(j)
            o.waits = keep
        counts = {e: 0 for e in ENGS}
        for o in ops:
            if o.signal and not o.dma:
                counts[o.eng] += 1
                o.count = counts[o.eng]
        self.sig_counts = counts
        esem = {e: nc.alloc_semaphore(f"es_{e}") for e in ENGS}
        dsem = [nc.alloc_semaphore(f"ds_{i}") for i in range(N_DSEM)]
        per_eng = {e: [o for o in ops if o.eng == e] for e in ENGS}
        stats = {e: [0, 0] for e in ENGS}

        def run(engname, eng):
            seen_e = {e: 0 for e in ENGS}
            seen_d = [0] * N_DSEM
            for o in per_eng[engname]:
                need_e = {}
                need_d = {}
                for j in o.waits:
                    p = ops[j]
                    if p.dma:
                        if p.dval > seen_d[p.dslot]:
                            need_d[p.dslot] = max(need_d.get(p.dslot, 0), p.dval)
                    else:
                        if p.count > seen_e[p.eng]:
                            need_e[p.eng] = max(need_e.get(p.eng, 0), p.count)
                for e, v in need_e.items():
                    eng.wait_ge(esem[e], v)
                    seen_e[e] = v
                    stats[engname][1] += 1
                for s, v in need_d.items():
                    eng.wait_ge(dsem[s], v)
                    seen_d[s] = v
                    stats[engname][1] += 1
                if o.fn is None:
                    continue
                ins = o.fn(eng)
                stats[engname][0] += 1
                if o.dma:
                    ins.then_inc(dsem[o.dslot], 16)
                elif o.signal:
                    ins.then_inc(esem[o.eng], 1)
        with nc.Block() as block:
            @block.tensor
            def _(e):
                run("pe", e)

            @block.scalar
            def _(e):
                run("act", e)

            @block.vector
            def _(e):
                run("dve", e)

            @block.gpsimd
            def _(e):
                run("pool", e)

            @block.sync
            def _(e):
                run("sp", e)
        self.stats = stats
        return stats

import numpy as np
import ml_dtypes
from concourse.bass_utils import run_bass_kernel_spmd

D = 1024
DFF = 2816
NFC = 22
DIN = 1840
EPS = 1e-6
NLT = 32
NCT = 2
NTT = 34
S = NTT * 128
TB = 6
B_SCALE = 96 ** -0.5
C_SCALE = 64 ** -0.5
DEPTH = 2


class Ctx:
    pass


class Arena:
    def __init__(self, nc, nbytes):
        self.t = nc.alloc_sbuf_tensor("s_arena", [128, nbytes // 2], BF16)
        self.nb = nbytes
        self.off = 0

    def reset(self):
        self.off = 0

    def alloc(self, shape, dt):
        n = 1
        for v in shape[1:]:
            n *= v
        nbytes = n * (4 if dt == F32 else 2)
        nbytes = (nbytes + 31) // 32 * 32
        assert self.off + nbytes <= self.nb, ("arena overflow", self.off, nbytes, self.nb)
        v = self.t[0:shape[0], self.off // 2:(self.off + nbytes) // 2]
        self.off += nbytes
        if dt == F32:
            v = v.bitcast(F32)
        v = v[:, 0:n]
        if len(shape) == 3:
            v = v.rearrange("p (a b) -> p a b", a=shape[1])
        elif len(shape) == 4:
            v = v.rearrange("p (a b c) -> p a b c", a=shape[1], b=shape[2])
        return v


def setup_common(nc, pg):
    c = Ctx()
    c.nc, c.pg = nc, pg
    A = lambda n, sh, dt: nc.alloc_sbuf_tensor("s_" + n, sh, dt)
    c.A = A
    c.ident_b = A("ident_b", [128, 128], BF16)
    c.ident_f = A("ident_f", [128, 128], F32)
    c.triu_f = A("triu_f", [128, 128], F32)
    c.tril_f = A("tril_f", [128, 128], F32)
    c.ones_f = A("ones_f", [128, 128], F32)
    c.triu_b = A("triu_b", [128, 128], BF16)
    c.tril_b = A("tril_b", [128, 128], BF16)
    c.sel = A("sel", [2, 2, 128], F32)
    c.epsb = A("epsb", [128, 1], F32)
    c.scT = A("scT", [128, 8, 2], BF16)
    c.cT = A("cT", [128, 2, 8], F32)
    c.gT = A("gT", [128, 3, 8], F32)
    c.modT = A("modT", [128, 9, 8, 2], F32)
    c.mulT = A("mulT", [128, 3, 8, 2], F32)
    c.gate = A("gate", [128, 3, 2, D], F32)
    c.adab = [A(f"adab{i}", [2, 256], F32) for i in range(2)]
    c.modrow = [A(f"modrow{i}", [2, 256], F32) for i in range(2)]
    c.g_cq = A("g_cq", [128, 256], F32)
    c.g_ckv = A("g_ckv", [128, 128], F32)
    c.g_q = A("g_q", [128, 96], F32)
    c.g_k = A("g_k", [128, 96], F32)
    c.g_qk8 = A("g_qk8", [128, 8, 64], F32)
    c.g_on = A("g_on", [128, 256], F32)
    c.gateb = A("gateb", [128, 16], F32)
    c.esink64 = A("esink64", [64, 6], F32)
    c.ones_b = A("ones_b", [128, 128], BF16)
    c.banks = [nc.alloc_psum_tensor(f"bank{i}", [128, 512], F32) for i in range(8)]
    return c


def load_consts(c, cst):
    pg = c.pg
    for n in ("ident_b", "ident_f", "triu_f", "tril_f", "ones_f", "triu_b", "tril_b", "sel"):
        pg.dma("sp", getattr(c, n)[:], cst[n], ["d:c_" + n], [n])
    pg.op("dve", lambda e: e.memset(c.epsb[:], EPS), [], ["epsb"])
    pg.op("dve", lambda e: e.memset(c.ones_b[:], 1.0), [], ["ones_b"])


def mod_setup(c, c2_ap):
    pg = c.pg
    for r in range(2):
        pg.dma("sp", c.cT[:, r, :], c2_ap[r, :].rearrange("(k p) -> p k", p=128), ["d:c2"], ["cT"],
               allow_slow_non_contiguous=True)
    for r in range(2):
        pg.op("act", lambda e, r=r: e.activation(out=c.scT[:, :, r], in_=c.cT[:, r, :], func=AF.Silu), ["cT"], ["scT"])


def mod_layer(c, adaw_ap, adab_ap, normg_ap, wbuf, wkeys, lname):
    pg = c.pg
    for n in range(3):
        pg.dma("sp", c.gT[:, n, :], normg_ap[n, :].rearrange("(k p) -> p k", p=128), ["d:normg" + lname], ["gT"],
               allow_slow_non_contiguous=True)
    W = 256
    adab2 = adab_ap.rearrange("(o n) -> o n", o=1)
    nblk = 9 * D // W
    per_v = D // W
    nb = len(wbuf)
    for blk in range(nblk):
        wb = wbuf[blk % nb]
        wk = wkeys[blk % nb]
        pg.dma("pool", wb, adaw_ap[:, blk * W:(blk + 1) * W].rearrange("(k p) n -> p k n", p=128),
               ["d:adaw" + lname], [wk])
        ab = c.adab[blk % 2]
        abk = f"adab{blk % 2}"
        for r in range(2):
            pg.dma("sp", ab[r:r + 1, :], adab2[:, blk * W:(blk + 1) * W], ["d:adab" + lname], [abk])
        bank = c.banks[blk % 2]
        bk = f"bank{blk % 2}"
        for k in range(8):
            pg.op("pe", lambda e, k=k, wb=wb, bank=bank: e.matmul(bank[0:2, 0:W], lhsT=c.scT[:, k, :], rhs=wb[:, k, :],
                                                                  start=(k == 0), stop=(k == 7)),
                  ["scT", wk], [bk])
        mr = c.modrow[blk % 2]
        mk = f"modrow{blk % 2}"
        pg.op("dve", lambda e, mr=mr, bank=bank, ab=ab: e.tensor_tensor(out=mr[:], in0=bank[0:2, 0:W],
                                                                        in1=ab[:], op=ALU.add),
              [bk, abk], [mk])
        v, part = blk // per_v, blk % per_v
        if v in (2, 5, 8):
            gi = v // 3
            scale = 1.0 if v == 5 else 0.5
            for s in range(2):
                b2 = c.banks[2 + s]
                b2k = f"bank{2 + s}"
                pg.op("pe", lambda e, b2=b2, s=s, mr=mr: e.matmul(b2[:, 0:W], lhsT=c.sel[:, s, :], rhs=mr[:],
                                                                  start=True, stop=True), ["sel", mk], [b2k])
                pg.op("act", lambda e, b2=b2, s=s, gi=gi, part=part, scale=scale: e.activation(
                    out=c.gate[:, gi, s, part * W:(part + 1) * W], in_=b2[:, 0:W], func=AF.Copy, scale=scale),
                    [b2k], ["gate"])
        else:
            b2 = c.banks[4 + (blk % 2)]
            b2k = f"bank{4 + (blk % 2)}"
            nq = W // 128
            for q in range(nq):
                pg.op("pe", lambda e, b2=b2, q=q, mr=mr: e.transpose(b2[:, 2 * q:2 * q + 2], mr[0:2, q * 128:(q + 1) * 128],
                                                                     c.ident_f[0:2, 0:2]), [mk, "ident_f"], [b2k])
            pg.op("dve", lambda e, b2=b2, v=v, part=part, nq=nq: e.tensor_copy(
                out=c.modT[:, v, part * nq:(part + 1) * nq, :], in_=b2[:, 0:2 * nq].rearrange("p (q s) -> p q s", s=2)),
                [b2k], ["modT"])
    for n in range(3):
        for s in range(2):
            pg.op("dve", lambda e, n=n, s=s: e.scalar_tensor_tensor(out=c.mulT[:, n, :, s], in0=c.modT[:, 1 + 3 * n, :, s],
                                                                    scalar=1.0, in1=c.gT[:, n, :], op0=ALU.add, op1=ALU.mult),
                  ["modT", "gT"], ["mulT"])


def ffn_layout(c):
    ar = c.ar
    ar.reset()
    b = {}
    b["x"] = ar.alloc([128, TB, D], F32)
    b["hT"] = ar.alloc([128, 8, TB * 128], BF16)
    b["actT"] = ar.alloc([128, NFC, TB * 128], BF16)
    b["wg"] = [ar.alloc([128, 8, 256], BF16) for i in range(3)]
    b["wu"] = [ar.alloc([128, 8, 256], BF16) for i in range(3)]
    b["wo"] = [ar.alloc([128, 2, 512], BF16) for i in range(4)]
    b["sg"] = [ar.alloc([128, TB * 128], BF16) for i in range(2)]
    b["tmp"] = [ar.alloc([128, 512], F32) for i in range(2)]
    b["junk"] = ar.alloc([128, D], BF16)
    b["xn"] = [ar.alloc([128, D], BF16) for i in range(2)]
    b["ssall"] = ar.alloc([128, 8], F32)
    b["ss"] = [ar.alloc([128, 1], F32) for i in range(2)]
    b["rstd"] = [ar.alloc([128, 1], F32) for i in range(2)]
    b["pst"] = [ar.alloc([128, DIN], F32) for i in range(2)]
    b["yT"] = ar.alloc([128, 8, TB * 128], BF16)
    b["wout"] = ar.alloc([128, 8, D], BF16)
    b["wi_cnt"] = 0
    b["wo_cnt"] = 0
    return b


def norm_mod_T(c, b, slots, isctx, n, tbank_base=6):
    pg = c.pg
    hT = b["hT"]
    x = b["x"]
    nt = len(slots)
    ssall = b["ssall"]
    for j, sl in enumerate(slots):
        pg.op("act", lambda e, sl=sl, j=j: e.activation(out=b["junk"], in_=x[:, sl, :], func=AF.Square, accum_out=ssall[:, j:j + 1]),
              [f"xb{sl}"], ["junk", f"ssall{j}"])
    SK = [f"ssall{j}" for j in range(nt)]
    pg.op("act", lambda e: e.activation(out=ssall[:, 0:nt], in_=ssall[:, 0:nt], func=AF.Sqrt, scale=1.0 / D, bias=c.epsb[:]),
          SK + ["epsb"], SK)
    pg.op("dve", lambda e: e.reciprocal(out=ssall[:, 0:nt], in_=ssall[:, 0:nt]), SK, SK)
    for j, sl in enumerate(slots):
        i = j % 2
        s = 1 if isctx[j] else 0
        pg.op("act", lambda e, sl=sl, i=i, j=j: e.activation(out=b["xn"][i], in_=x[:, sl, :], func=AF.Copy, scale=ssall[:, j:j + 1]),
              [f"xb{sl}", f"ssall{j}"], [f"xn{i}"])
        bank = c.banks[tbank_base + i]
        bk = f"bank{tbank_base + i}"
        bb = bank[:, :].bitcast(BF16)
        for k in range(8):
            pg.op("pe", lambda e, k=k, i=i, bb=bb: e.transpose(bb[:, k * 128:(k + 1) * 128], b["xn"][i][:, k * 128:(k + 1) * 128],
                                                               c.ident_b[:, :]), [f"xn{i}", "ident_b"], [bk])
        for k in range(8):
            if k % 2 == 0:
                pg.op("act", lambda e, k=k, bb=bb, j=j, n=n, s=s: e.activation(
                    out=hT[:, k, j * 128:(j + 1) * 128], in_=bb[:, k * 128:(k + 1) * 128], func=AF.Identity,
                    scale=c.mulT[:, n, k, s:s + 1], bias=c.modT[:, 3 * n, k, s:s + 1]),
                    [bk, "mulT", "modT"], [f"hT{j}"])
            else:
                pg.op("dve", lambda e, k=k, bb=bb, j=j, n=n, s=s: e.tensor_scalar(
                    out=hT[:, k, j * 128:(j + 1) * 128], in0=bb[:, k * 128:(k + 1) * 128],
                    scalar1=c.mulT[:, n, k, s:s + 1], scalar2=c.modT[:, 3 * n, k, s:s + 1], op0=ALU.mult, op1=ALU.add),
                    [bk, "mulT", "modT"], [f"hT{j}"])


def ffn_block(c, b, slots, isctx, n, gi, wi_ap, wo_ap, lname, cached=False):
    pg = c.pg
    nt = len(slots)
    NTOK = nt * 128
    hT, actT = b["hT"], b["actT"]
    x = b["x"]
    norm_mod_T(c, b, slots, isctx, n)
    hkeys = [f"hT{j}" for j in range(nt)]
    halves = []
    off = 0
    while off < NTOK:
        w = min(512, NTOK - off)
        halves.append((off, w))
        off += w
    for jb in range(11):
        st = b["wi_cnt"] % 3
        b["wi_cnt"] += 1
        wg, wu = b["wg"][st], b["wu"][st]
        if cached:
            pg.dma("sp", wg, wi_ap[(jb * 2) * 128:(jb * 2 + 1) * 128, :].rearrange("p (k n) -> p k n", k=8),
                   ["d:wi" + lname], [f"wg{st}"])
            pg.dma("sp", wu, wi_ap[(jb * 2 + 1) * 128:(jb * 2 + 2) * 128, :].rearrange("p (k n) -> p k n", k=8),
                   ["d:wi" + lname], [f"wu{st}"])
        else:
            pg.dma("pool", wg, wi_ap[:, jb * 256:(jb + 1) * 256].rearrange("(k p) n -> p k n", p=128),
                   ["d:wi" + lname], [f"wg{st}"])
            pg.dma("pool", wu, wi_ap[:, DFF + jb * 256:DFF + (jb + 1) * 256].rearrange("(k p) n -> p k n", p=128),
                   ["d:wi" + lname], [f"wu{st}"])
        for m in range(2):
            ch = jb * 2 + m
            p = ch % 2
            gbanks = [0 + 2 * p, 1 + 2 * p]
            ub = [4, 5]
            for k in range(8):
                for hi, (o0, w) in enumerate(halves):
                    pg.op("pe", lambda e, k=k, hi=hi, o0=o0, w=w, wg=wg, m=m, gbanks=gbanks: e.matmul(
                        c.banks[gbanks[hi]][:, 0:w], lhsT=wg[:, k, m * 128:(m + 1) * 128], rhs=hT[:, k, o0:o0 + w],
                        start=(k == 0), stop=(k == 7)), [f"wg{st}"] + hkeys, [f"bank{gbanks[hi]}"])
            for k in range(8):
                for hi, (o0, w) in enumerate(halves):
                    pg.op("pe", lambda e, k=k, hi=hi, o0=o0, w=w, wu=wu, m=m, ub=ub: e.matmul(
                        c.banks[ub[hi]][:, 0:w], lhsT=wu[:, k, m * 128:(m + 1) * 128], rhs=hT[:, k, o0:o0 + w],
                        start=(k == 0), stop=(k == 7)), [f"wu{st}"] + hkeys, [f"bank{ub[hi]}"])
            sg = b["sg"][p]
            for hi, (o0, w) in enumerate(halves):
                pg.op("act", lambda e, hi=hi, o0=o0, w=w, sg=sg, gbanks=gbanks: e.activation(
                    out=sg[:, o0:o0 + w], in_=c.banks[gbanks[hi]][:, 0:w], func=AF.Silu),
                    [f"bank{gbanks[hi]}"], [f"sg{p}_{hi}"])
                pg.op("dve", lambda e, hi=hi, o0=o0, w=w, sg=sg, ub=ub, ch=ch: e.tensor_tensor(
                    out=actT[:, ch, o0:o0 + w], in0=c.banks[ub[hi]][:, 0:w], in1=sg[:, o0:o0 + w], op=ALU.mult),
                    [f"bank{ub[hi]}", f"sg{p}_{hi}"], [f"actT{ch}"])
    for oh in range(2):
        for cg in range(11):
            st = b["wo_cnt"] % 4
            b["wo_cnt"] += 1
            wo = b["wo"][st]
            if cached:
                pg.dma("sp", wo, wo_ap[(oh * 11 + cg) * 128:(oh * 11 + cg + 1) * 128, :].rearrange("p (c n) -> p c n", c=2),
                       ["d:wo" + lname], [f"wo{st}"])
            else:
                pg.dma("pool", wo, wo_ap[cg * 256:(cg + 1) * 256, oh * 512:(oh + 1) * 512].rearrange("(c p) n -> p c n", p=128),
                       ["d:wo" + lname], [f"wo{st}"])
            for cc in range(2):
                ch = cg * 2 + cc
                for j in range(nt):
                    pg.op("pe", lambda e, j=j, ch=ch, cc=cc, wo=wo: e.matmul(
                        c.banks[j][:, :], lhsT=actT[:, ch, j * 128:(j + 1) * 128], rhs=wo[:, cc, :],
                        start=(ch == 0), stop=(ch == NFC - 1)), [f"actT{ch}", f"wo{st}"], [f"bank{j}"])
        for j, sl in enumerate(slots):
            s = 1 if isctx[j] else 0
            tmp = b["tmp"][j % 2]
            pg.op("dve", lambda e, j=j, s=s, tmp=tmp, oh=oh: e.tensor_tensor(
                out=tmp, in0=c.banks[j][:, :], in1=c.gate[:, gi, s, oh * 512:(oh + 1) * 512], op=ALU.mult),
                [f"bank{j}", "gate"], [f"tmp{j % 2}"])
            pg.op("pool", lambda e, sl=sl, tmp=tmp, oh=oh: e.tensor_tensor(
                out=x[:, sl, oh * 512:(oh + 1) * 512], in0=x[:, sl, oh * 512:(oh + 1) * 512], in1=tmp, op=ALU.add),
                [f"tmp{j % 2}", f"xb{sl}"], [f"xb{sl}"])


ACTK = [f"actT{ch}" for ch in range(NFC)]


def win_block(c, b, tiles, isctx, win_ap, ps_ap, lname):
    pg = c.pg
    nt = len(tiles)
    wv = b["actT"].rearrange("p a b -> p (a b)")[:, 0:8 * DIN].rearrange("p (k n) -> p k n", k=8)
    pg.dma("pool", wv, win_ap.rearrange("(k p) n -> p k n", p=128), ["d:win" + lname], ACTK)
    norm_mod_T(c, b, list(range(nt)), isctx, 1)
    cols = [(0, 512), (512, 512), (1024, 512), (1536, DIN - 1536)]
    for j, t in enumerate(tiles):
        pst = b["pst"][j % 2]
        for ci, (c0, w) in enumerate(cols):
            bank = c.banks[ci + 4 * (j % 2)] if ci + 4 * (j % 2) < 6 else c.banks[ci - 2 + 4 * (j % 2) - 4]
            bi = (ci + 4 * (j % 2)) % 6
            bank = c.banks[bi]
            for k in range(8):
                pg.op("pe", lambda e, k=k, j=j, c0=c0, w=w, bank=bank: e.matmul(
                    bank[:, 0:w], lhsT=b["hT"][:, k, j * 128:(j + 1) * 128], rhs=wv[:, k, c0:c0 + w],
                    start=(k == 0), stop=(k == 7)), ACTK + [f"hT{j}"], [f"bank{bi}"])
            pg.op("act" if ci % 2 == 0 else "dve",
                  (lambda e, pst=pst, c0=c0, w=w, bank=bank: e.activation(out=pst[:, c0:c0 + w], in_=bank[:, 0:w], func=AF.Copy))
                  if ci % 2 == 0 else
                  (lambda e, pst=pst, c0=c0, w=w, bank=bank: e.tensor_copy(out=pst[:, c0:c0 + w], in_=bank[:, 0:w])),
                  [f"bank{bi}"], [f"pst{j % 2}_{ci}"])
        pg.dma("sp", ps_ap[t * 128:(t + 1) * 128, :], pst, [f"pst{j % 2}_{ci}" for ci in range(4)], [f"d:ps{t}"])


def wout_block(c, b, tiles, isctx, YT_ap, lname):
    pg = c.pg
    nt = len(tiles)
    yT = b["yT"]
    t0 = tiles[0]
    pg.dma("sp", yT[:, :, 0:nt * 128], YT_ap[:, t0 * 128:(t0 + nt) * 128].rearrange("(k p) t -> p k t", p=128),
           ["d:YT"], ["yTb"])
    x = b["x"]
    for j in range(nt):
        s = 1 if isctx[j] else 0
        for oh in range(2):
            bi = (2 * j + oh) % 6
            bank = c.banks[bi]
            for k in range(8):
                pg.op("pe", lambda e, k=k, j=j, oh=oh, bank=bank: e.matmul(
                    bank[:, :], lhsT=yT[:, k, j * 128:(j + 1) * 128], rhs=b["wout"][:, k, oh * 512:(oh + 1) * 512],
                    start=(k == 0), stop=(k == 7)), ["yTb", "wout"], [f"bank{bi}"])
            tmp = b["tmp"][oh]
            pg.op("dve", lambda e, s=s, tmp=tmp, oh=oh, bank=bank: e.tensor_tensor(
                out=tmp, in0=bank[:, :], in1=c.gate[:, 1, s, oh * 512:(oh + 1) * 512], op=ALU.mult),
                [f"bank{bi}", "gate"], [f"tmp{oh}"])
            pg.op("pool", lambda e, j=j, tmp=tmp, oh=oh: e.tensor_tensor(
                out=x[:, j, oh * 512:(oh + 1) * 512], in0=x[:, j, oh * 512:(oh + 1) * 512], in1=tmp, op=ALU.add),
                [f"tmp{oh}", f"xb{j}"], [f"xb{j}"])


def cast_weights_wi(c, src_ap, dst_ap, key):
    pg = c.pg
    for jb in range(11):
        for gu in range(2):
            c0 = gu * DFF + jb * 256
            r0 = (jb * 2 + gu) * 128
            pg.dma("pool", dst_ap[r0:r0 + 128, :].rearrange("p (k n) -> p k n", k=8),
                   src_ap[:, c0:c0 + 256].rearrange("(k p) n -> p k n", p=128), [], ["d:" + key])


def cast_weights_wo(c, src_ap, dst_ap, key):
    pg = c.pg
    for oh in range(2):
        for cg in range(11):
            r0 = (oh * 11 + cg) * 128
            pg.dma("pool", dst_ap[r0:r0 + 128, :].rearrange("p (c n) -> p c n", c=2),
                   src_ap[cg * 256:(cg + 1) * 256, oh * 512:(oh + 1) * 512].rearrange("(c p) n -> p c n", p=128), [], ["d:" + key])

def mixer_layout(c):
    ar = c.ar
    ar.reset()
    m = {}
    m["Vr"] = ar.alloc([128, NTT, 6, 65], BF16)
    m["GV"] = ar.alloc([128, NTT, 2, 65], BF16)
    m["GKT"] = ar.alloc([64, 2, S], BF16)
    m["pt"] = [ar.alloc([128, DIN], F32) for i in range(2)]
    m["rt"] = [ar.alloc([128, 192], F32) for i in range(2)]
    m["junk"] = ar.alloc([128, 256], F32)
    m["s1"] = ar.alloc([128, 8], F32)
    m["s6"] = ar.alloc([128, 8], F32)
    m["s8"] = ar.alloc([128, 8], F32)
    m["s4"] = ar.alloc([128, 8], F32)
    m["s9"] = ar.alloc([128, 8], F32)
    m["sq"] = ar.alloc([128, 576], F32)
    m["ckvn"] = ar.alloc([128, 128], BF16)
    m["ckvnT"] = ar.alloc([128, 128], BF16)
    m["kvs"] = ar.alloc([128, 6, 128], F32)
    m["tmp6"] = ar.alloc([128, 6, 64], F32)
    m["krg"] = ar.alloc([128, 32], F32)
    m["krr"] = ar.alloc([128, 32], F32)
    m["krt"] = ar.alloc([128, 32], F32)
    m["krc"] = ar.alloc([128, 32], F32)
    m["kf"] = ar.alloc([128, 6, 96], BF16)
    m["ktT"] = ar.alloc([96, 6, 128], BF16)
    m["cqn"] = ar.alloc([128, 256], BF16)
    m["cqnT"] = ar.alloc([128, 2, 128], BF16)
    m["qs"] = ar.alloc([128, 6, 96], F32)
    m["qn"] = ar.alloc([128, 6, 96], F32)
    m["tr"] = ar.alloc([128, 6, 32], F32)
    m["tc"] = ar.alloc([128, 6, 32], F32)
    m["qf"] = ar.alloc([128, 6, 96], BF16)
    m["qtT"] = ar.alloc([96, 6, 128], BF16)
    m["n8"] = ar.alloc([128, 8, 64], F32)
    m["t8"] = ar.alloc([128, 8, 64], F32)
    m["c8"] = ar.alloc([128, 8, 64], F32)
    m["g8"] = ar.alloc([128, 8, 64], BF16)
    m["gqT"] = ar.alloc([64, 6, 128], BF16)
    m["mqk"] = ar.alloc([128, 256], BF16)
    m["mT"] = ar.alloc([32, 8, 128], BF16)
    m["gt"] = ar.alloc([128, 16], F32)
    m["ge"] = ar.alloc([128, 2, 4], F32)
    m["wukv"] = ar.alloc([128, 768], BF16)
    m["wuq"] = ar.alloc([128, 2, 576], BF16)
    m["scan"] = []
    for d in range(2):
        sb = {}
        sb["mTi"] = [ar.alloc([32, 8, 128], BF16) for i in range(2)]
        sb["ktok"] = [ar.alloc([128, 128], BF16) for i in range(2)]
        sb["gti"] = [ar.alloc([128, 16], F32) for i in range(2)]
        sb["v"] = [ar.alloc([128, 256], F32) for i in range(2)]
        sb["rF"] = ar.alloc([128, 8], F32)
        sb["du"] = ar.alloc([128, 4], F32)
        sb["u"] = ar.alloc([128, 4], F32)
        sb["vpp"] = ar.alloc([128, 4, 65], BF16)
        sb["PTm"] = ar.alloc([128, 4, 128], BF16)
        sb["C"] = ar.alloc([32, 4, 65], F32)
        sb["Cb"] = ar.alloc([32, 4, 65], BF16)
        sb["t1"] = ar.alloc([128, 4, 65], F32)
        sb["den"] = ar.alloc([128, 4], F32)
        sb["dneg"] = ar.alloc([128, 4], F32)
        sb["hd"] = ar.alloc([128, 4, 64], F32)
        m["scan"].append(sb)
    m["hfi"] = [m["pt"][0][:, 256 * i:256 * (i + 1)] for i in range(2)]
    m["hbi"] = [m["pt"][0][:, 512 + 256 * i:512 + 256 * (i + 1)] for i in range(2)]
    m["oi"] = [m["pt"][0][:, 1024 + 256 * i:1024 + 256 * (i + 1)] for i in range(2)]
    m["hs"] = ar.alloc([128, 4, 64], F32)
    m["hq"] = ar.alloc([128, 4, 64], F32)
    m["sgo"] = ar.alloc([128, 256], F32)
    m["ya"] = ar.alloc([128, 256], BF16)
    m["yaT"] = ar.alloc([128, 2, 128], BF16)
    m["KTh"] = [ar.alloc([96, S], BF16) for i in range(1)]
    m["QTh"] = [ar.alloc([96, S], BF16) for i in range(1)]
    m["PT"] = [ar.alloc([128, 512], BF16) for i in range(4)]
    m["rden"] = [ar.alloc([64, 512], F32) for i in range(2)]
    m["accs"] = [ar.alloc([65, 512], F32) for i in range(2)]
    m["yh"] = [ar.alloc([64, 512], BF16) for i in range(2)]
    m["q3"] = [ar.alloc([64, 3, 128], BF16) for i in range(4)]
    return m


def load_layer_small(c, m, W, l):
    pg = c.pg

    def bc(dst, src, key):
        pg.dma("sp", dst, src.partition_broadcast(128), ["d:" + key], [key])
    bc(c.g_cq[:], W["mla_cq_norm"][l], "g_cq")
    bc(c.g_ckv[:], W["mla_ckv_norm"][l], "g_ckv")
    bc(c.g_q[:], W["mla_q_norm"][l], "g_q")
    bc(c.g_k[:], W["mla_k_norm"][l], "g_k")
    for h in range(6):
        bc(c.g_qk8[:, h, :], W["gqa_q_norm"][l], "g_qk8")
    for h in range(6, 8):
        bc(c.g_qk8[:, h, :], W["gqa_k_norm"][l], "g_qk8")
    bc(c.g_on[:], W["mlstm_out_norm"][l], "g_on")
    bc(c.gateb[:], W["mlstm_gate_b"][l], "gateb")
    pg.dma("sp", c.esink64[:], W["gqa_sink"][l].partition_broadcast(64), ["d:sink"], ["esink64"])
    pg.op("act", lambda e: e.activation(out=c.esink64[:], in_=c.esink64[:], func=AF.Exp), ["esink64"], ["esink64"])
    pg.dma("pool", m["wukv"], W["mla_w_ukv"][l], ["d:wukv"], ["wukv"])
    pg.dma("pool", m["wuq"], W["mla_w_uq"][l].rearrange("(k p) n -> p k n", p=128), ["d:wuq"], ["wuq"])
    pg.op("dve", lambda e: e.memset(m["Vr"][:, :, :, 64:65], 1.0), [], ["Vr1"])
    pg.op("dve", lambda e: e.memset(m["GV"][:, :, :, 64:65], 1.0), [], ["GV1"])


def rstd_multi(c, m, src, H, d, sq_tmp, ss, key_src, key_out, extra=None, sqkey="sqtmp"):
    pg = c.pg
    pg.op("act", lambda e: e.activation(out=sq_tmp, in_=src, func=AF.Square), key_src, [sqkey])
    pg.op("dve", lambda e: e.tensor_reduce(out=ss[:, 0:H], in_=sq_tmp, axis=AX.X, op=ALU.add), [sqkey], [key_out])
    if extra is not None:
        ex, exk = extra
        pg.op("dve", lambda e: e.tensor_scalar(out=ss[:, 0:H], in0=ss[:, 0:H], scalar1=ex, scalar2=None, op0=ALU.add),
              [key_out, exk], [key_out])
    pg.op("act", lambda e: e.activation(out=ss[:, 0:H], in_=ss[:, 0:H], func=AF.Ln, scale=1.0 / d, bias=c.epsb[:]),
          [key_out, "epsb"], [key_out])
    pg.op("act", lambda e: e.activation(out=ss[:, 0:H], in_=ss[:, 0:H], func=AF.Exp, scale=-0.5), [key_out], [key_out])


def rstd_single(c, m, src, d, ss, key_src, key_out, scale_d=None):
    pg = c.pg
    pg.op("act", lambda e: e.activation(out=m["junk"][:, 0:d], in_=src, func=AF.Square, accum_out=ss), key_src, ["junk", key_out])
    pg.op("act", lambda e: e.activation(out=ss, in_=ss, func=AF.Ln, scale=1.0 / d, bias=c.epsb[:]),
          [key_out, "epsb"], [key_out])
    pg.op("act", lambda e: e.activation(out=ss, in_=ss, func=AF.Exp, scale=-0.5), [key_out], [key_out])


def rope_multi(c, src, dst, tmp, tmpc, cos, sins, H, R, ksrc, kdst, ktab, kt="ropetmp", kc="ropetmpc"):
    pg = c.pg
    Q = R // 4
    for q in range(4):
        q2 = q ^ 1
        pg.op("dve", lambda e, q=q, q2=q2: e.tensor_tensor(
            out=tmp[:, :, q * Q:(q + 1) * Q], in0=src[:, :, q2 * Q:(q2 + 1) * Q],
            in1=sins[:, None, q * Q:(q + 1) * Q].broadcast_to([128, H, Q]), op=ALU.mult), ksrc + [ktab], [kt])
    pg.op("dve", lambda e: e.tensor_tensor(out=tmpc, in0=src, in1=cos[:, None, :].broadcast_to([128, H, R]), op=ALU.mult),
          ksrc + [ktab], [kc])
    pg.op("dve", lambda e: e.tensor_tensor(out=dst, in0=tmpc, in1=tmp, op=ALU.add), [kt, kc], kdst)


def prep_tile(c, m, t, D_, need_ctx):
    pg = c.pg
    isctx = t >= NLT
    i = t % 2
    pt = m["pt"][i]
    ptk = f"pt{i}"
    rt = m["rt"][i]
    rtk = f"rt{i}"
    pg.begin_capture()
    pg.dma("sp", pt, D_["ps"][t * 128:(t + 1) * 128, :], [f"d:ps{t}"], [ptk])
    if not isctx:
        pg.dma("sp", rt, D_["rope"][t * 128:(t + 1) * 128, :], ["d:rope"], [rtk])
    TB_ = c.banks[7][:, :].bitcast(BF16)
    pg.mark()
    s1 = m["s1"]
    rstd_single(c, m, pt[:, 1040:1168], 128, s1[:, 0:1], [ptk], "s1a")
    pg.op("dve", lambda e: e.scalar_tensor_tensor(out=m["ckvn"], in0=pt[:, 1040:1168], scalar=s1[:, 0:1], in1=c.g_ckv[:],
                                                  op0=ALU.mult, op1=ALU.mult), [ptk, "s1a", "g_ckv"], ["ckvn"])
    pg.op("pe", lambda e: e.transpose(TB_[:, 0:128], m["ckvn"], c.ident_b[:, :]), ["ckvn", "ident_b"], ["bank7"])
    pg.op("act", lambda e: e.activation(out=m["ckvnT"], in_=TB_[:, 0:128], func=AF.Copy), ["bank7"], ["ckvnT"])
    pg.op("pe", lambda e: e.matmul(c.banks[0][:, 0:512], lhsT=m["ckvnT"], rhs=m["wukv"][:, 0:512], start=True, stop=True),
          ["ckvnT", "wukv"], ["bank0"])
    pg.op("pe", lambda e: e.matmul(c.banks[1][:, 0:256], lhsT=m["ckvnT"], rhs=m["wukv"][:, 512:768], start=True, stop=True),
          ["ckvnT", "wukv"], ["bank1"])
    kvs = m["kvs"]
    pg.op("act", lambda e: e.activation(out=kvs[:, 0:4, :], in_=c.banks[0][:, 0:512].rearrange("p (h j) -> p h j", h=4), func=AF.Copy),
          ["bank0"], ["kvsA"])
    pg.op("act", lambda e: e.activation(out=kvs[:, 4:6, :], in_=c.banks[1][:, 0:256].rearrange("p (h j) -> p h j", h=2), func=AF.Copy),
          ["bank1"], ["kvsB"])
    KV = ["kvsA", "kvsB"]
    pg.op("dve", lambda e: e.tensor_copy(out=m["Vr"][:, t, :, 0:64], in_=kvs[:, :, 64:128]), KV, [f"Vr{t}"])
    pg.op("act", lambda e: e.activation(out=m["junk"][:, 0:32], in_=pt[:, 1168:1200], func=AF.Square, accum_out=s1[:, 1:2]),
          [ptk], ["junk", "s1b"])
    rstd_multi(c, m, kvs[:, :, 0:64], 6, 96, m["tmp6"], m["s6"], KV, "s6", extra=(s1[:, 1:2], "s1b"), sqkey="tmp6b")
    rk = m["s6"]
    pg.op("dve", lambda e: e.tensor_tensor(out=m["tmp6"], in0=kvs[:, :, 0:64], in1=rk[:, 0:6].unsqueeze(2).broadcast_to([128, 6, 64]),
                                           op=ALU.mult), KV + ["s6"], ["tmp6b"])
    pg.op("dve", lambda e: e.tensor_tensor(out=m["kf"][:, :, 0:64], in0=m["tmp6"], in1=c.g_k[:, None, 0:64].broadcast_to([128, 6, 64]),
                                           op=ALU.mult), ["tmp6b", "g_k"], ["kfA"])
    pg.op("dve", lambda e: e.tensor_tensor(out=m["krg"], in0=pt[:, 1168:1200], in1=c.g_k[:, 64:96], op=ALU.mult), [ptk, "g_k"], ["krg"])
    if not isctx:
        rope_multi(c, m["krg"].unsqueeze(1), m["krr"].unsqueeze(1), m["krt"].unsqueeze(1), m["krc"].unsqueeze(1),
                   rt[:, 0:32], rt[:, 32:64], 1, 32, ["krg"], ["krr"], rtk, kt="ropetmpK", kc="ropetmpcK")
        krr, krrk = m["krr"], "krr"
    else:
        krr, krrk = m["krg"], "krg"
    pg.op("dve", lambda e: e.tensor_tensor(out=m["kf"][:, :, 64:96], in0=krr[:, None, :].broadcast_to([128, 6, 32]),
                                           in1=rk[:, 0:6].unsqueeze(2).broadcast_to([128, 6, 32]), op=ALU.mult),
          [krrk, "s6"], ["kfB"])
    for h in range(6):
        pg.op("pe", lambda e, h=h: e.transpose(TB_[0:96, h * 128:(h + 1) * 128], m["kf"][:, h, :], c.ident_b[:, :]),
              ["kfA", "kfB", "ident_b"], ["bank7"])
    pg.op("act", lambda e: e.activation(out=m["ktT"], in_=TB_[0:96, 0:768].rearrange("p (h t) -> p h t", h=6), func=AF.Copy),
          ["bank7"], ["ktT"])
    pg.dma("sp", D_["KT"][:, :, t * 128:(t + 1) * 128].rearrange("h d t -> d h t"), m["ktT"], ["ktT"], ["d:KT"])
    pg.mark()
    if (not isctx) or need_ctx:
        rstd_single(c, m, pt[:, 784:1040], 256, s1[:, 2:3], [ptk], "s1c")
        pg.op("dve", lambda e: e.scalar_tensor_tensor(out=m["cqn"], in0=pt[:, 784:1040], scalar=s1[:, 2:3], in1=c.g_cq[:],
                                                      op0=ALU.mult, op1=ALU.mult), [ptk, "s1c", "g_cq"], ["cqn"])
        TB6 = c.banks[6][:, :].bitcast(BF16)
        for kk in range(2):
            pg.op("pe", lambda e, kk=kk: e.transpose(TB6[:, kk * 128:(kk + 1) * 128], m["cqn"][:, kk * 128:(kk + 1) * 128], c.ident_b[:, :]),
                  ["cqn", "ident_b"], ["bank6"])
        pg.op("act", lambda e: e.activation(out=m["cqnT"], in_=TB6[:, 0:256].rearrange("p (k t) -> p k t", k=2), func=AF.Copy),
              ["bank6"], ["cqnT"])
        for kk in range(2):
            pg.op("pe", lambda e, kk=kk: e.matmul(c.banks[2][:, 0:512], lhsT=m["cqnT"][:, kk, :], rhs=m["wuq"][:, kk, 0:512],
                                                  start=(kk == 0), stop=(kk == 1)), ["cqnT", "wuq"], ["bank2"])
        for kk in range(2):
            pg.op("pe", lambda e, kk=kk: e.matmul(c.banks[3][:, 0:64], lhsT=m["cqnT"][:, kk, :], rhs=m["wuq"][:, kk, 512:576],
                                                  start=(kk == 0), stop=(kk == 1)), ["cqnT", "wuq"], ["bank3"])
        qsf = m["qs"].rearrange("p h j -> p (h j)")
        pg.op("act", lambda e: e.activation(out=qsf[:, 0:512], in_=c.banks[2][:, 0:512], func=AF.Copy), ["bank2"], ["qsA"])
        pg.op("act", lambda e: e.activation(out=qsf[:, 512:576], in_=c.banks[3][:, 0:64], func=AF.Copy), ["bank3"], ["qsB"])
        QS = ["qsA", "qsB"]
        rstd_multi(c, m, m["qs"], 6, 96, m["sq"][:, 0:576].rearrange("p (h j) -> p h j", h=6), m["s8"], QS, "s8q")
        rq = m["s8"]
        pg.op("dve", lambda e: e.tensor_tensor(out=m["qn"], in0=m["qs"], in1=rq[:, 0:6].unsqueeze(2).broadcast_to([128, 6, 96]),
                                               op=ALU.mult), QS + ["s8q", "sqtmp"], ["qn"])
        pg.op("dve", lambda e: e.tensor_tensor(out=m["qn"], in0=m["qn"], in1=c.g_q[:, None, :].broadcast_to([128, 6, 96]),
                                               op=ALU.mult), ["qn", "g_q"], ["qn"])
        if not isctx:
            pg.op("dve", lambda e: e.tensor_copy(out=m["qf"][:, :, 0:64], in_=m["qn"][:, :, 0:64]), ["qn"], ["qfA"])
            rope_multi(c, m["qn"][:, :, 64:96], m["qf"][:, :, 64:96], m["tr"], m["tc"], rt[:, 0:32], rt[:, 32:64], 6, 32,
                       ["qn"], ["qfB"], rtk)
        else:
            pg.op("dve", lambda e: e.tensor_copy(out=m["qf"], in_=m["qn"]), ["qn"], ["qfA", "qfB"])
        for h in range(6):
            pg.op("pe", lambda e, h=h: e.transpose(TB6[0:96, h * 128:(h + 1) * 128], m["qf"][:, h, :], c.ident_b[:, :]),
                  ["qfA", "qfB", "ident_b"], ["bank6"])
        pg.op("act", lambda e: e.activation(out=m["qtT"], in_=TB6[0:96, 0:768].rearrange("p (h t) -> p h t", h=6), func=AF.Copy),
              ["bank6"], ["qtT"])
        pg.dma("sp", D_["QT"][:, :, t * 128:(t + 1) * 128].rearrange("h d t -> d h t"), m["qtT"], ["qtT"], ["d:QT"])
    pg.mark()
    qk = pt[:, 1200:1712].rearrange("p (h j) -> p h j", h=8)
    rstd_multi(c, m, qk, 8, 64, m["t8"], m["s9"], [ptk], "s8g", sqkey="ropetmp8")
    pg.op("dve", lambda e: e.tensor_tensor(out=m["n8"], in0=qk, in1=m["s9"][:, 0:8].unsqueeze(2).broadcast_to([128, 8, 64]),
                                           op=ALU.mult), [ptk, "s8g", "ropetmp8"], ["n8"])
    pg.op("dve", lambda e: e.tensor_tensor(out=m["n8"], in0=m["n8"], in1=c.g_qk8[:], op=ALU.mult), ["n8", "g_qk8"], ["n8"])
    if not isctx:
        rope_multi(c, m["n8"], m["g8"], m["t8"], m["c8"], rt[:, 64:128], rt[:, 128:192], 8, 64, ["n8"], ["g8"], rtk,
                   kt="ropetmp8", kc="ropetmpc8")
    else:
        pg.op("dve", lambda e: e.tensor_copy(out=m["g8"], in_=m["n8"]), ["n8"], ["g8"])
    TB5 = c.banks[5][:, :].bitcast(BF16)
    for h in range(8):
        pg.op("pe", lambda e, h=h: e.transpose(TB5[0:64, h * 128:(h + 1) * 128], m["g8"][:, h, :], c.ident_b[:, :]),
              ["g8", "ident_b"], ["bank5"])
    pg.op("act", lambda e: e.activation(out=m["gqT"], in_=TB5[0:64, 0:768].rearrange("p (h t) -> p h t", h=6), func=AF.Copy),
          ["bank5"], ["gqT"])
    pg.op("act", lambda e: e.activation(out=m["GKT"][:, :, t * 128:(t + 1) * 128],
                                        in_=TB5[0:64, 768:1024].rearrange("p (h t) -> p h t", h=2), func=AF.Copy),
          ["bank5"], [f"GKT{t}"])
    pg.dma("sp", D_["GQT"][:, :, t * 128:(t + 1) * 128].rearrange("h d t -> d h t"), m["gqT"], ["gqT"], ["d:GQT"])
    pg.op("dve", lambda e: e.tensor_copy(out=m["GV"][:, t, :, 0:64], in_=pt[:, 1712:1840].rearrange("p (h j) -> p h j", h=2)),
          [ptk], [f"GV{t}"])
    pg.mark()
    pg.op("act", lambda e: e.activation(out=m["mqk"][:, 0:128], in_=pt[:, 0:128], func=AF.Copy, scale=32 ** -0.5), [ptk], ["mqkA"])
    pg.op("dve", lambda e: e.tensor_copy(out=m["mqk"][:, 128:256], in_=pt[:, 128:256]), [ptk], ["mqkB"])
    TB4 = c.banks[4][:, :].bitcast(BF16)
    for j in range(8):
        pg.op("pe", lambda e, j=j: e.transpose(TB4[0:32, j * 128:(j + 1) * 128], m["mqk"][:, j * 32:(j + 1) * 32], c.ident_b[:, :]),
              ["mqkA", "mqkB", "ident_b"], ["bank4"])
    pg.op("act", lambda e: e.activation(out=m["mT"], in_=TB4[0:32, 0:1024].rearrange("p (j t) -> p j t", j=8), func=AF.Copy),
          ["bank4"], ["mT"])
    pg.dma("sp", D_["MQKT"][:, :, t * 128:(t + 1) * 128].rearrange("j d t -> d j t"), m["mT"], ["mT"], ["d:MQKT"])
    pg.dma("sp", D_["MKtok"][t * 128:(t + 1) * 128, :], m["mqk"][:, 128:256], ["mqkB"], ["d:MKtok"])
    gt = m["gt"]
    pg.op("dve", lambda e: e.tensor_tensor(out=gt, in0=pt[:, 768:784], in1=c.gateb[:], op=ALU.add), [ptk, "gateb"], ["gt"])
    gv = gt.rearrange("p (a b) -> p a b", a=2)[:, :, 4:8]
    pg.op("act", lambda e: e.activation(out=m["ge"], in_=gv, func=AF.Exp, scale=-1.0), ["gt"], ["ge"])
    pg.op("act", lambda e: e.activation(out=m["ge"], in_=m["ge"], func=AF.Ln, bias=1.0), ["ge"], ["ge"])
    pg.op("dve", lambda e: e.tensor_scalar(out=gv, in0=m["ge"], scalar1=-1.0, scalar2=None, op0=ALU.mult), ["ge", "gt"], ["gt"])
    pg.dma("sp", D_["G"][t * 128:(t + 1) * 128, :], gt, ["gt"], ["d:G"])
    pg.end_capture(head=1, group=1)


def mlstm_scan_gen(c, m, D_, d, need_ctx):
    pg = c.pg
    sb = m["scan"][d]
    order = [32, 33] + list(range(32)) if d == 0 else [33, 32] + list(range(31, -1, -1))
    X, Y = c.banks[4 + 2 * d], c.banks[5 + 2 * d]
    Xk, Yk = f"bank{4 + 2 * d}", f"bank{5 + 2 * d}"
    K = lambda n: f"{n}_{d}"
    C, Cb = sb["C"], sb["Cb"]
    pg.op("dve", lambda e: e.memset(C, 0.0), [], [K("C")])
    pg.op("dve", lambda e: e.memset(Cb, 0.0), [], [K("Cb")])
    tri_f = c.triu_f if d == 0 else c.tril_f
    tri_b = c.triu_b if d == 0 else c.tril_b
    trik = ("triu_f", "triu_b") if d == 0 else ("tril_f", "tril_b")
    HD = D_["HF"] if d == 0 else D_["HB"]

    def loads(n_):
        t = order[n_]
        i = n_ % 2
        rows = slice(t * 128, (t + 1) * 128)
        pg.dma("sp", sb["mTi"][i], D_["MQKT"][:, :, rows].rearrange("j d t -> d j t"), ["d:MQKT"], [K(f"mTi{i}")])
        pg.dma("sp", sb["ktok"][i], D_["MKtok"][rows, :], ["d:MKtok"], [K(f"ktok{i}")])
        pg.dma("sp", sb["gti"][i], D_["G"][rows, :], ["d:G"], [K(f"gti{i}")])
        pg.dma("sp", sb["v"][i], D_["ps"][rows, 256:512], [f"d:ps{t}"], [K(f"v{i}")])
    loads(0)
    yield
    for n_, t in enumerate(order):
        i = n_ % 2
        isctx = t >= NLT
        want_out = (not isctx) or need_ctx
        mTi, ktok, gti, v = sb["mTi"][i], sb["ktok"][i], sb["gti"][i], sb["v"][i]
        rows = slice(t * 128, (t + 1) * 128)
        if n_ + 1 < len(order):
            loads(n_ + 1)
        ig = gti[:, 8 * d:8 * d + 4]
        lf = gti[:, 8 * d + 4:8 * d + 8]
        pg.op("pe", lambda e, lf=lf: e.matmul(Y[:, 0:4], lhsT=tri_f[:, :], rhs=lf, start=True, stop=True), [K(f"gti{i}"), trik[0]], [Yk])
        pg.op("pe", lambda e, lf=lf: e.matmul(Y[:, 4:8], lhsT=c.ones_f[:, :], rhs=lf, start=True, stop=True), [K(f"gti{i}"), "ones_f"], [Yk])
        yield
        yield
        pg.op("act", lambda e: e.activation(out=sb["rF"], in_=Y[:, 0:8], func=AF.Exp), [Yk], [K("rF")])
        pg.op("dve", lambda e, ig=ig: e.tensor_tensor(out=sb["du"], in0=ig, in1=Y[:, 0:4], op=ALU.subtract), [K(f"gti{i}"), Yk], [K("du")])
        yield
        yield
        pg.op("act", lambda e: e.activation(out=sb["u"], in_=sb["du"], func=AF.Exp), [K("du")], [K("u")])
        for h in range(4):
            pg.op("pe", lambda e, h=h, mTi=mTi: e.matmul(X[:, h * 128:(h + 1) * 128], lhsT=mTi[:, 4 + h, :], rhs=mTi[:, h, :],
                                                         start=True, stop=True), [K(f"mTi{i}")], [Xk])
        yield
        yield
        pg.op("dve", lambda e, v=v: e.tensor_tensor(out=sb["vpp"][:, :, 0:64], in0=v.rearrange("p (h j) -> p h j", h=4),
                                                    in1=sb["u"].unsqueeze(2).broadcast_to([128, 4, 64]), op=ALU.mult),
              [K(f"v{i}"), K("u")], [K("vppA")])
        pg.op("dve", lambda e: e.tensor_copy(out=sb["vpp"][:, :, 64:65], in_=sb["u"].unsqueeze(2)), [K("u")], [K("vppB")])
        pg.op("dve", lambda e: e.tensor_tensor(out=sb["PTm"], in0=X[:, :].rearrange("p (h t) -> p h t", h=4),
                                               in1=tri_b[:, None, :].broadcast_to([128, 4, 128]), op=ALU.mult),
              [Xk, trik[1]], [K("PTm")])
        yield
        yield
        for h in range(4):
            pg.op("pe", lambda e, h=h: e.matmul(Y[:, 8 + h * 65:8 + (h + 1) * 65], lhsT=sb["PTm"][:, h, :], rhs=sb["vpp"][:, h, :],
                                                start=True, stop=False), [K("PTm"), K("vppA"), K("vppB")], [Yk])
            pg.op("pe", lambda e, h=h, mTi=mTi: e.matmul(Y[:, 8 + h * 65:8 + (h + 1) * 65], lhsT=mTi[:, h, :], rhs=Cb[:, h, :],
                                                         start=False, stop=True), [K(f"mTi{i}"), K("Cb")], [Yk])
        for h in range(4):
            pg.op("pe", lambda e, h=h, ktok=ktok: e.matmul(X[0:32, h * 65:(h + 1) * 65], lhsT=ktok[:, h * 32:(h + 1) * 32],
                                                           rhs=sb["vpp"][:, h, :], start=True, stop=True),
                  [K(f"ktok{i}"), K("vppA"), K("vppB")], [Xk])
        yield
        yield
        pg.op("dve", lambda e: e.tensor_tensor(out=C, in0=C, in1=X[0:32, 0:260].rearrange("p (h j) -> p h j", h=4), op=ALU.add),
              [K("C"), Xk], [K("C")])
        pg.op("dve", lambda e: e.tensor_tensor(out=C, in0=C, in1=sb["rF"][0:32, 4:8].unsqueeze(2).broadcast_to([32, 4, 65]), op=ALU.mult),
              [K("C"), K("rF")], [K("C")])
        pg.op("dve", lambda e: e.tensor_copy(out=Cb, in_=C), [K("C")], [K("Cb")])
        if not want_out:
            yield
            continue
        pg.op("dve", lambda e: e.tensor_tensor(out=sb["t1"], in0=Y[:, 8:268].rearrange("p (h j) -> p h j", h=4),
                                               in1=sb["rF"][:, 0:4].unsqueeze(2).broadcast_to([128, 4, 65]), op=ALU.mult),
              [Yk, K("rF")], [K("t1")])
        yield
        pg.op("dve", lambda e: e.tensor_scalar(out=sb["dneg"], in0=sb["t1"][:, :, 64], scalar1=-1.0, scalar2=None, op0=ALU.mult),
              [K("t1")], [K("dneg")])
        pg.op("dve", lambda e: e.tensor_tensor(out=sb["den"], in0=sb["t1"][:, :, 64], in1=sb["dneg"], op=ALU.max),
              [K("t1"), K("dneg")], [K("den")])
        pg.op("dve", lambda e: e.tensor_single_scalar(out=sb["den"], in_=sb["den"], scalar=1.0, op=ALU.max), [K("den")], [K("den")])
        pg.op("dve", lambda e: e.reciprocal(out=sb["den"], in_=sb["den"]), [K("den")], [K("den")])
        yield
        pg.op("dve", lambda e: e.tensor_tensor(out=sb["hd"], in0=sb["t1"][:, :, 0:64],
                                               in1=sb["den"].unsqueeze(2).broadcast_to([128, 4, 64]), op=ALU.mult),
              [K("t1"), K("den")], [K("hd")])
        pg.dma("sp", HD[rows, :], sb["hd"].rearrange("p h j -> p (h j)"), [K("hd")], ["d:HF" if d == 0 else "d:HB"])
        yield


def mlstm_combine(c, m, D_, need_ctx):
    pg = c.pg
    tiles = list(range(NTT)) if need_ctx else list(range(NLT))

    def loads(n_):
        t = tiles[n_]
        i = n_ % 2
        rows = slice(t * 128, (t + 1) * 128)
        pg.dma("sp", m["hfi"][i], D_["HF"][rows, :], ["d:HF"], [f"hfi{i}"])
        pg.dma("sp", m["hbi"][i], D_["HB"][rows, :], ["d:HB"], [f"hbi{i}"])
        pg.dma("sp", m["oi"][i], D_["ps"][rows, 512:768], [f"d:ps{t}"], [f"oi{i}"])
    pg.op("sp", None, [], ["pt0"])
    loads(0)
    for n_, t in enumerate(tiles):
        i = n_ % 2
        rows = slice(t * 128, (t + 1) * 128)
        if n_ + 1 < len(tiles):
            loads(n_ + 1)
        pg.op("dve", lambda e, i=i: e.tensor_tensor(out=m["hs"].rearrange("p h j -> p (h j)"), in0=m["hfi"][i], in1=m["hbi"][i], op=ALU.add),
              [f"hfi{i}", f"hbi{i}"], ["hs"])
        rstd_multi(c, m, m["hs"], 4, 64, m["sq"][:, 0:256].rearrange("p (h j) -> p h j", h=4), m["s4"], ["hs"], "s4h")
        pg.op("dve", lambda e: e.tensor_tensor(out=m["hq"], in0=m["hs"], in1=m["s4"][:, 0:4].unsqueeze(2).broadcast_to([128, 4, 64]),
                                               op=ALU.mult), ["hs", "s4h", "sqtmp"], ["hq"])
        pg.op("dve", lambda e: e.tensor_tensor(out=m["hq"], in0=m["hq"], in1=c.g_on[:].rearrange("p (h j) -> p h j", h=4),
                                               op=ALU.mult), ["hq", "g_on"], ["hq"])
        pg.op("act", lambda e, i=i: e.activation(out=m["sgo"], in_=m["oi"][i], func=AF.Exp, scale=-1.0), [f"oi{i}"], ["sgo"])
        pg.op("act", lambda e: e.activation(out=m["sgo"], in_=m["sgo"], func=AF.Ln, bias=1.0), ["sgo"], ["sgo"])
        pg.op("act", lambda e: e.activation(out=m["sgo"], in_=m["sgo"], func=AF.Exp, scale=-1.0), ["sgo"], ["sgo"])
        pg.op("dve", lambda e: e.tensor_tensor(out=m["ya"], in0=m["hq"].rearrange("p h j -> p (h j)"), in1=m["sgo"], op=ALU.mult),
              ["hq", "sgo"], ["ya"])
        TB_ = c.banks[7][:, :].bitcast(BF16)
        for kk in range(2):
            pg.op("pe", lambda e, kk=kk: e.transpose(TB_[:, kk * 128:(kk + 1) * 128], m["ya"][:, kk * 128:(kk + 1) * 128], c.ident_b[:, :]),
                  ["ya", "ident_b"], ["bank7"])
        pg.op("act", lambda e: e.activation(out=m["yaT"], in_=TB_[:, 0:256].rearrange("p (k t) -> p k t", k=2), func=AF.Copy),
              ["bank7"], ["yaT"])
        pg.dma("sp", D_["YT"][0:256, rows].rearrange("(k p) t -> p k t", p=128), m["yaT"], ["yaT"], ["d:YT"])


def run_tasks(tasks):
    tasks = list(tasks)
    while tasks:
        for g in list(tasks):
            try:
                next(g)
            except StopIteration:
                tasks.remove(g)


def attn_fin(c, m, bo, bd, bko, bkd, nq, ai, esink_ap=None, nh=1):
    pg = c.pg
    rd = m["rden"][ai]
    rk = f"rden{ai}"
    if esink_ap is not None:
        pg.op("dve", lambda e: e.tensor_tensor(out=rd[:, 0:nq].rearrange("p (h t) -> p h t", h=nh),
                                               in0=bd[0:64, 0:nq].rearrange("p (h t) -> p h t", h=nh),
                                               in1=esink_ap.unsqueeze(2).broadcast_to([64, nh, nq // nh]), op=ALU.add),
              [bkd, "esink64"], [rk])
        pg.op("dve", lambda e: e.reciprocal(out=rd[:, 0:nq], in_=rd[:, 0:nq]), [rk], [rk])
    else:
        pg.op("dve", lambda e: e.reciprocal(out=rd[:, 0:nq], in_=bd[0:64, 0:nq]), [bkd], [rk])
    yh = m["yh"][ai]
    pg.op("dve", lambda e: e.tensor_tensor(out=yh[:, 0:nq], in0=bo[0:64, 0:nq], in1=rd[:, 0:nq], op=ALU.mult),
          [bko, rk], [f"yh{ai}"])
    return yh


def run_units(units, look=2):
    n = len(units)
    for u in range(n + look):
        if u < n:
            units[u][0]()
        if u - look >= 0:
            units[u - look][1]()


def run_units_gen(units, look=1):
    n = len(units)
    for u in range(n + look):
        if u < n:
            units[u][0]()
        if u - look >= 0:
            units[u - look][1]()
        yield


def mla_attention_gen(c, m, D_, need_ctx, wide=True):
    pg = c.pg
    units = []
    cnt = 0
    qcnt = 0
    for h in range(6):
        KTh, QTh = m["KTh"][0], m["QTh"][0]
        blocks = [(qb * 512, 512, list(range(NTT))) for qb in range(8)]
        if need_ctx:
            blocks.append((4096, 256, [32, 33]))
        first = True
        for (q0, nq, kts) in blocks:
            ai = qcnt % 2
            qcnt += 1
            ob = 4 + ai
            for n_, kt in enumerate(kts):
                r = cnt % 4
                cnt += 1

                def stage_a(h=h, q0=q0, nq=nq, kt=kt, r=r, ld=first):
                    if ld:
                        pg.dma("sp", KTh, D_["KT"][h], ["d:KT"], ["KTh"])
                        pg.dma("sp", QTh, D_["QT"][h], ["d:QT"], ["QTh"])
                    bs = c.banks[r]
                    pg.op("pe", lambda e: e.matmul(bs[:, 0:nq], lhsT=KTh[:, kt * 128:(kt + 1) * 128], rhs=QTh[:, q0:q0 + nq],
                                                   start=True, stop=True), ["KTh", "QTh"], [f"bank{r}"])
                    PT = m["PT"][r]
                    pg.op("act", lambda e: e.activation(out=PT[:, 0:nq], in_=bs[:, 0:nq], func=AF.Exp, scale=B_SCALE),
                          [f"bank{r}"], [f"PT{r}"])

                def stage_b(h=h, q0=q0, nq=nq, kt=kt, r=r, n_=n_, L=len(kts), ob=ob, ai=ai):
                    bo = c.banks[ob]
                    PT = m["PT"][r]
                    pg.op("pe", lambda e: e.matmul(bo[0:65, 0:nq], lhsT=m["Vr"][:, kt, h, :], rhs=PT[:, 0:nq],
                                                   start=(n_ == 0), stop=(n_ == L - 1)), [f"PT{r}", f"Vr{kt}", "Vr1"], [f"bank{ob}"])
                    if n_ == L - 1:
                        accs = m["accs"][ai]
                        ak = f"accs{ai}"
                        pg.op("act", lambda e: e.activation(out=accs[:, 0:nq], in_=bo[0:65, 0:nq], func=AF.Copy), [f"bank{ob}"], [ak])
                        pg.op("dve", lambda e: e.reciprocal(out=accs[64:65, 0:nq], in_=accs[64:65, 0:nq]), [ak], [ak])

                def stage_c(h=h, q0=q0, nq=nq, ai=ai):
                    accs = m["accs"][ai]
                    ak = f"accs{ai}"
                    bb = c.banks[6 + ai]
                    pg.op("pe", lambda e: e.matmul(bb[0:64, 0:nq], lhsT=c.ones_f[64:65, 0:64], rhs=accs[64:65, 0:nq],
                                                   start=True, stop=True), [ak, "ones_f"], [f"bank{6 + ai}"])
                    yh = m["yh"][ai]
                    pg.op("dve", lambda e: e.tensor_tensor(out=yh[:, 0:nq], in0=bb[0:64, 0:nq], in1=accs[0:64, 0:nq], op=ALU.mult),
                          [ak, f"bank{6 + ai}"], [f"yh{ai}"])
                    pg.dma("sp", D_["YT"][256 + h * 64:256 + (h + 1) * 64, q0:q0 + nq], yh[:, 0:nq], [f"yh{ai}"], ["d:YT"])
                units.append((stage_a, stage_b, stage_c if n_ == len(kts) - 1 else None))
                first = False
    n = len(units)
    LOOK, DEF = 2, 8
    for u in range(n + LOOK + DEF + 1):
        if u < n:
            units[u][0]()
        if 0 <= u - LOOK < n:
            units[u - LOOK][1]()
        if 0 <= u - LOOK - DEF < n and units[u - LOOK - DEF][2] is not None:
            units[u - LOOK - DEF][2]()
        yield


def gqa_attention(c, m, D_, need_ctx):
    pg = c.pg
    units = []
    cnt = 0
    qcnt = 0
    nblk = NTT if need_ctx else NLT
    for g in range(2):
        for ib in range(nblk):
            qi = qcnt % 4
            ai = qcnt % 2
            qcnt += 1
            q3 = m["q3"][qi]
            if ib >= NLT:
                kts = [32, 33]
            else:
                kts = [32, 33] + ([ib - 1] if ib > 0 else []) + [ib] + ([ib + 1] if ib < NLT - 1 else [])
            for n_, kt in enumerate(kts):
                r = cnt % 4
                cnt += 1

                def stage_a(g=g, ib=ib, kt=kt, r=r, n_=n_, q3=q3, qi=qi, bidx=qcnt - 1):
                    if n_ == 0:
                        todo = [bidx + 2] if bidx > 0 else [0, 1, 2]
                        for bb in todo:
                            if bb < 2 * nblk:
                                g2, ib2 = bb // nblk, bb % nblk
                                pg.dma("sp", m["q3"][bb % 4],
                                       D_["GQT"][3 * g2:3 * g2 + 3, :, ib2 * 128:(ib2 + 1) * 128].rearrange("h d t -> d h t"),
                                       ["d:GQT"], [f"q3{bb % 4}"])
                    bs = c.banks[r]
                    pg.op("pe", lambda e: e.matmul(bs[:, 0:384], lhsT=m["GKT"][:, g, kt * 128:(kt + 1) * 128],
                                                   rhs=q3.rearrange("p h t -> p (h t)"), start=True, stop=True),
                          [f"GKT{kt}", f"q3{qi}"], [f"bank{r}"])
                    PT = m["PT"][r]
                    pg.op("act", lambda e: e.activation(out=PT[:, 0:384], in_=bs[:, 0:384], func=AF.Exp, scale=C_SCALE),
                          [f"bank{r}"], [f"PT{r}"])
                    msk = None
                    if ib < NLT and kt == ib - 1:
                        msk, mk = c.tril_b, "tril_b"
                    if ib < NLT and kt == ib + 1:
                        msk, mk = c.triu_b, "triu_b"
                    if msk is not None:
                        pg.op("dve", lambda e: e.tensor_tensor(out=PT[:, 0:384].rearrange("p (h t) -> p h t", h=3),
                                                               in0=PT[:, 0:384].rearrange("p (h t) -> p h t", h=3),
                                                               in1=msk[:, None, :].broadcast_to([128, 3, 128]), op=ALU.mult),
                              [f"PT{r}", mk], [f"PT{r}"])

                def stage_b(g=g, ib=ib, kt=kt, r=r, n_=n_, L=len(kts), ai=ai):
                    bo, bd = c.banks[4 + ai], c.banks[6 + ai]
                    PT = m["PT"][r]
                    pg.op("pe", lambda e: e.matmul(bo[0:64, 0:384], lhsT=m["GV"][:, kt, g, 0:64], rhs=PT[:, 0:384],
                                                   start=(n_ == 0), stop=(n_ == L - 1)), [f"PT{r}", f"GV{kt}"], [f"bank{4 + ai}"])
                    pg.op("pe", lambda e: e.matmul(bd[:, 0:384], lhsT=c.ones_b[:, :], rhs=PT[:, 0:384],
                                                   start=(n_ == 0), stop=(n_ == L - 1)), [f"PT{r}", "ones_b"], [f"bank{6 + ai}"])
                    if n_ == L - 1:
                        yh = attn_fin(c, m, bo, bd, f"bank{4 + ai}", f"bank{6 + ai}", 384, ai,
                                      esink_ap=c.esink64[:, 3 * g:3 * g + 3], nh=3)
                        for j in range(3):
                            hh = 3 * g + j
                            pg.dma("sp", D_["YT"][640 + hh * 64:640 + (hh + 1) * 64, ib * 128:(ib + 1) * 128],
                                   yh[:, j * 128:(j + 1) * 128], [f"yh{ai}"], ["d:YT"])
                units.append((stage_a, stage_b))
    run_units(units)

WNAMES = ["ada_w", "ada_b", "norm_g", "ffn1_wi", "ffn1_wo", "ffn2_wi", "ffn2_wo", "w_in", "w_out",
          "mlstm_gate_b", "mlstm_out_norm", "mla_cq_norm", "mla_ckv_norm", "mla_w_uq", "mla_w_ukv",
          "mla_q_norm", "mla_k_norm", "gqa_q_norm", "gqa_k_norm", "gqa_sink"]
WSHAPES = {"ada_w": [2, 1024, 9216], "ada_b": [2, 9216], "norm_g": [2, 3, 1024], "ffn1_wi": [2, 1024, 5632],
           "ffn1_wo": [2, 2816, 1024], "ffn2_wi": [2, 1024, 5632], "ffn2_wo": [2, 2816, 1024], "w_in": [2, 1024, 1840],
           "w_out": [2, 1024, 1024], "mlstm_gate_b": [2, 16], "mlstm_out_norm": [2, 256], "mla_cq_norm": [2, 256],
           "mla_ckv_norm": [2, 128], "mla_w_uq": [2, 256, 576], "mla_w_ukv": [2, 128, 768], "mla_q_norm": [2, 96],
           "mla_k_norm": [2, 96], "gqa_q_norm": [2, 64], "gqa_k_norm": [2, 64], "gqa_sink": [2, 6]}


def make_consts():
    ident = np.eye(128, dtype=np.float32)
    triu = np.triu(np.ones((128, 128), np.float32))
    tril = np.tril(np.ones((128, 128), np.float32))
    sel = np.zeros((2, 2, 128), np.float32)
    sel[0, 0, :] = 1
    sel[1, 1, :] = 1
    bf = ml_dtypes.bfloat16
    cst = {"ident_b": ident.astype(bf), "ident_f": ident, "triu_f": triu, "tril_f": tril,
           "ones_f": np.ones((128, 128), np.float32), "triu_b": triu.astype(bf), "tril_b": tril.astype(bf), "sel": sel}
    T = 4096
    pos = np.arange(T)
    row = (pos // 64).astype(np.float32)
    col = (pos % 64).astype(np.float32)

    def tab(half):
        inv = (10000.0 ** (-np.arange(half, dtype=np.float32) / half)).astype(np.float32)
        ar_, ac_ = row[:, None] * inv, col[:, None] * inv
        cr, sr, cc, sc = np.cos(ar_), np.sin(ar_), np.cos(ac_), np.sin(ac_)
        return np.concatenate([cr, cr, cc, cc], 1), np.concatenate([-sr, sr, -sc, sc], 1)
    c32, s32 = tab(8)
    c64, s64 = tab(16)
    cst["rope"] = np.concatenate([c32, s32, c64, s64], 1).astype(np.float32)
    return cst


CSHAPES = {"ident_b": ([128, 128], "bf"), "ident_f": ([128, 128], "f"), "triu_f": ([128, 128], "f"), "tril_f": ([128, 128], "f"),
           "ones_f": ([128, 128], "f"), "triu_b": ([128, 128], "bf"), "tril_b": ([128, 128], "bf"), "sel": ([2, 2, 128], "f"),
           "rope": ([4096, 192], "f")}


def blocks_of(tiles):
    return [tiles[i:i + TB] for i in range(0, len(tiles), TB)]


def build_program(debug=False):
    nc = bass.Bass("TRN2", target_bir_lowering=False)
    pg = Prog(nc)
    x_in = nc.dram_tensor("x_in", [4096, D], F32, kind="ExternalInput").ap()
    ctx_in = nc.dram_tensor("ctx_in", [256, D], F32, kind="ExternalInput").ap()
    c2 = nc.dram_tensor("c2", [2, D], F32, kind="ExternalInput").ap()
    W = {n: nc.dram_tensor(n, WSHAPES[n], F32, kind="ExternalInput").ap() for n in WNAMES}
    cst = {n: nc.dram_tensor("k_" + n, sh, F32 if ty == "f" else BF16, kind="ExternalInput").ap() for n, (sh, ty) in CSHAPES.items()}
    out = nc.dram_tensor("out", [4096, D], F32, kind="ExternalOutput").ap()
    kind = "ExternalOutput" if debug else "Internal"
    D_ = {}
    D_["xs"] = nc.dram_tensor("xs", [S, D], F32, kind=kind).ap()
    D_["ps"] = nc.dram_tensor("ps", [S, DIN], F32, kind=kind).ap()
    D_["KT"] = nc.dram_tensor("KT", [6, 96, S], BF16, kind="Internal").ap()
    D_["QT"] = nc.dram_tensor("QT", [6, 96, S], BF16, kind="Internal").ap()
    D_["GQT"] = nc.dram_tensor("GQT", [6, 64, S], BF16, kind="Internal").ap()
    D_["MQKT"] = nc.dram_tensor("MQKT", [8, 32, S], BF16, kind="Internal").ap()
    D_["MKtok"] = nc.dram_tensor("MKtok", [S, 128], BF16, kind="Internal").ap()
    D_["G"] = nc.dram_tensor("G", [S, 16], F32, kind="Internal").ap()
    D_["HF"] = nc.dram_tensor("HF", [S, 256], F32, kind="Internal").ap()
    D_["HB"] = nc.dram_tensor("HB", [S, 256], F32, kind="Internal").ap()
    D_["YT"] = nc.dram_tensor("YT", [D, S], BF16, kind=kind).ap()
    D_["rope"] = cst["rope"]
    WB = {}
    for l in range(DEPTH):
        for f in (1, 2):
            if (l, f) == (0, 1):
                continue
            WB[(l, f, "wi")] = nc.dram_tensor(f"wibf{l}{f}", [22 * 128, 2048], BF16, kind="Internal").ap()
            WB[(l, f, "wo")] = nc.dram_tensor(f"wobf{l}{f}", [22 * 128, 1024], BF16, kind="Internal").ap()

    def do_cast(l, f):
        cast_weights_wi(c, W[f"ffn{f}_wi"][l], WB[(l, f, "wi")], f"wi{l}{f}")
        cast_weights_wo(c, W[f"ffn{f}_wo"][l], WB[(l, f, "wo")], f"wo{l}{f}")

    c = setup_common(nc, pg)
    c.ar = Arena(nc, nc.sbuf_bytes_remaining - 2048)
    load_consts(c, cst)
    mod_setup(c, c2)

    def src_rows(l, t):
        if l == 0:
            return x_in[t * 128:(t + 1) * 128, :] if t < NLT else ctx_in[(t - NLT) * 128:(t - NLT + 1) * 128, :]
        return D_["xs"][t * 128:(t + 1) * 128, :]

    for l in range(DEPTH):
        need_ctx = l < DEPTH - 1
        ln = f"L{l}"
        pg.barrier()
        b = ffn_layout(c)
        mod_layer(c, W["ada_w"][l], W["ada_b"][l], W["norm_g"][l], b["wg"] + b["wu"],
                  [f"wg{i}" for i in range(3)] + [f"wu{i}" for i in range(3)], ln)
        for tiles in blocks_of(list(range(NTT))):
            nt = len(tiles)
            isctx = [t >= NLT for t in tiles]
            for j, t in enumerate(tiles):
                pg.dma("sp", b["x"][:, j, :], src_rows(l, t), [f"d:xs{t}"], [f"xb{j}"])
            if l == 0:
                ffn_block(c, b, list(range(nt)), isctx, 0, 0, W["ffn1_wi"][l], W["ffn1_wo"][l], ln + "a")
            else:
                ffn_block(c, b, list(range(nt)), isctx, 0, 0, WB[(l, 1, "wi")], WB[(l, 1, "wo")], f"{l}1", cached=True)
            win_block(c, b, tiles, isctx, W["w_in"][l], D_["ps"], ln)
            for j, t in enumerate(tiles):
                pg.dma("sp", D_["xs"][t * 128:(t + 1) * 128, :], b["x"][:, j, :], [f"xb{j}"], [f"d:xs{t}"])
        pg.barrier()
        m = mixer_layout(c)
        load_layer_small(c, m, W, l)
        do_cast(l, 2)
        if l + 1 < DEPTH:
            do_cast(l + 1, 1)
        for t in range(NTT):
            prep_tile(c, m, t, D_, need_ctx)
        import os
        if os.environ.get("MIXMODE", "2") == "1":
            run_tasks([mlstm_scan_gen(c, m, D_, 0, need_ctx)])
            run_tasks([mlstm_scan_gen(c, m, D_, 1, need_ctx)])
            run_tasks([mla_attention_gen(c, m, D_, need_ctx)])
        elif os.environ.get("MIXMODE", "2") == "2":
            run_tasks([mlstm_scan_gen(c, m, D_, 0, need_ctx), mlstm_scan_gen(c, m, D_, 1, need_ctx)])
            run_tasks([mla_attention_gen(c, m, D_, need_ctx)])
        else:
            run_tasks([mla_attention_gen(c, m, D_, need_ctx, wide=False), mlstm_scan_gen(c, m, D_, 0, need_ctx),
                       mlstm_scan_gen(c, m, D_, 1, need_ctx)])
        gqa_attention(c, m, D_, need_ctx)
        mlstm_combine(c, m, D_, need_ctx)
        pg.barrier()
        b = ffn_layout(c)
        pg.dma("pool", b["wout"], W["w_out"][l].rearrange("(k p) n -> p k n", p=128), ["d:wout" + ln], ["wout"])
        tl = list(range(NTT)) if need_ctx else list(range(NLT))
        for tiles in blocks_of(tl):
            nt = len(tiles)
            isctx = [t >= NLT for t in tiles]
            for j, t in enumerate(tiles):
                pg.dma("sp", b["x"][:, j, :], D_["xs"][t * 128:(t + 1) * 128, :], [f"d:xs{t}"], [f"xb{j}"])
            wout_block(c, b, tiles, isctx, D_["YT"], ln)
            ffn_block(c, b, list(range(nt)), isctx, 2, 2, WB[(l, 2, "wi")], WB[(l, 2, "wo")], f"{l}2", cached=True)
            for j, t in enumerate(tiles):
                if l == DEPTH - 1:
                    pg.dma("sp", out[t * 128:(t + 1) * 128, :], b["x"][:, j, :], [f"xb{j}"], ["d:out"])
                else:
                    pg.dma("sp", D_["xs"][t * 128:(t + 1) * 128, :], b["x"][:, j, :], [f"xb{j}"], [f"d:xs{t}"])
    pg.op("sp", None, ["d:out"], [])
    pg.barrier()
    st = pg.emit()
    return nc, st


_CACHE = {}


def kernel(**inputs):
    if "nc" not in _CACHE:
        _CACHE["nc"] = build_program()[0]
    nc = _CACHE["nc"]
    cst = make_consts()
    f32 = lambda a: np.ascontiguousarray(np.asarray(a, dtype=np.float32))
    x = f32(inputs["x"])
    ctx = f32(inputs["ctx"])
    cvec = f32(inputs["c"])
    cctx = f32(inputs["c_ctx"])
    in_maps = []
    for b_ in range(4):
        im = {"x_in": x[b_], "ctx_in": ctx[b_], "c2": np.stack([cvec[b_], cctx])}
        for n in WNAMES:
            im[n] = f32(inputs[n])
        for n, v in cst.items():
            im["k_" + n] = v
        in_maps.append(im)
    res = run_bass_kernel_spmd(nc, in_maps, core_ids=list(range(4)))
    return np.stack([res.results[b_]["out"] for b_ in range(4)]).astype(np.float32)
```

```python
import concourse.bass as bass
import concourse.mybir as mybir

F32 = mybir.dt.float32
BF16 = mybir.dt.bfloat16
AF = mybir.ActivationFunctionType
ALU = mybir.AluOpType
AX = mybir.AxisListType

ENGS = ("pe", "act", "dve", "pool", "sp")
N_DSEM = 24
DSLOTS = {"sp": (0, 14), "pool": (14, 8), "act": (22, 2)}


class Op:
    __slots__ = ("eng", "fn", "reads", "writes", "dma", "idx", "waits", "signal", "count", "dslot", "dval", "extra")

    def __init__(self, eng, fn, reads, writes, dma):
        self.eng = eng
        self.fn = fn
        self.reads = reads
        self.writes = writes
        self.dma = dma
        self.waits = []
        self.signal = False
        self.count = 0
        self.extra = None


class Prog:
    def __init__(self, nc, same_engine_sync=True):
        self.nc = nc
        self.ops = []
        self.same_engine_sync = same_engine_sync
        self._cap = None

    def op(self, eng, fn, reads=(), writes=(), dma=False):
        o = Op(eng, fn, tuple(reads), tuple(writes), dma)
        if self._cap is not None:
            self._cap[-1].append(o)
            return o
        o.idx = len(self.ops)
        self.ops.append(o)
        return o

    def begin_capture(self):
        self._cap = [[]]

    def mark(self):
        self._cap.append([])

    def end_capture(self, head=1, group=1):
        lists = self._cap
        self._cap = None

        def ins(o):
            o.idx = len(self.ops)
            self.ops.append(o)
        for l in lists[:head]:
            for o in l:
                ins(o)
        rest = [l for l in lists[head:] if l]
        pos = [0] * len(rest)
        active = True
        while active:
            active = False
            for k, l in enumerate(rest):
                for _ in range(group):
                    if pos[k] < len(l):
                        ins(l[pos[k]])
                        pos[k] += 1
                        active = True

    def barrier(self):
        for i, e in enumerate(ENGS):
            o = self.op(e, None)
            o.extra = "last" if i == len(ENGS) - 1 else "bar"

    def dma(self, eng, out, in_, reads, writes, **kw):
        return self.op(eng, lambda e: e.dma_start(out=out, in_=in_, **kw), reads, writes, dma=True)

    def emit(self):
        nc = self.nc
        ops = self.ops
        last_w = {}
        readers = {}
        n_dma = 0
        dma_ops = []
        last_eng = {}
        dma_since = []
        dma_q = {}
        for o in ops:
            deps = set()
            for r in o.reads:
                j = last_w.get(r)
                if j is not None:
                    deps.add(j)
            for w in o.writes:
                j = last_w.get(w)
                if j is not None:
                    deps.add(j)
                rd = readers.get(w)
                if rd:
                    for j in rd.values():
                        deps.add(j)
            if o.dma:
                base, ns = DSLOTS[o.eng]
                lst = dma_q.setdefault(o.eng, [])
                nd = len(lst)
                o.dslot = base + nd % ns
                o.dval = 16 * (nd // ns + 1)
                if nd >= ns:
                    deps.add(lst[nd - ns].idx)
                lst.append(o)
                n_dma += 1
            deps.discard(o.idx)
            o.waits = sorted(deps)
            if o.extra:
                for e2, j in last_eng.items():
                    deps.add(j)
                for j in dma_since:
                    deps.add(j)
                deps.discard(o.idx)
                o.waits = sorted(deps)
                if o.extra == "last":
                    dma_since = []
            if o.fn is not None:
                for w in o.writes:
                    last_w[w] = o.idx
                    readers[w] = {}
                rk = ("dma", o.idx) if o.dma else o.eng
                for r in o.reads:
                    if r not in o.writes:
                        readers.setdefault(r, {})[rk] = o.idx
            if o.fn is not None:
                if o.dma:
                    dma_since.append(o.idx)
                else:
                    last_eng[o.eng] = o.idx
        for o in ops:
            keep = []
            for j in o.waits:
                p = ops[j]
                if p.dma:
                    keep.append(j)
                    continue
                if p.eng == o.eng and not o.dma:
                    if p.eng == "pe" or not self.same_engine_sync:
                        continue
                p.signal = True
                keep.append(j)
            o.waits = keep
        counts = {e: 0 for e in ENGS}
        for o in ops:
            if o.signal and not o.dma:
                counts[o.eng] += 1
                o.count = counts[o.eng]
        self.sig_counts = counts
        esem = {e: nc.alloc_semaphore(f"es_{e}") for e in ENGS}
        dsem = [nc.alloc_semaphore(f"ds_{i}") for i in range(N_DSEM)]
        per_eng = {e: [o for o in ops if o.eng == e] for e in ENGS}
        stats = {e: [0, 0] for e in ENGS}

        def run(engname, eng):
            seen_e = {e: 0 for e in ENGS}
            seen_d = [0] * N_DSEM
            for o in per_eng[engname]:
                need_e = {}
                need_d = {}
                for j in o.waits:
                    p = ops[j]
                    if p.dma:
                        if p.dval > seen_d[p.dslot]:
                            need_d[p.dslot] = max(need_d.get(p.dslot, 0), p.dval)
                    else:
                        if p.count > seen_e[p.eng]:
                            need_e[p.eng] = max(need_e.get(p.eng, 0), p.count)
                for e, v in need_e.items():
                    eng.wait_ge(esem[e], v)
                    seen_e[e] = v
                    stats[engname][1] += 1
                for s, v in need_d.items():
                    eng.wait_ge(dsem[s], v)
                    seen_d[s] = v
                    stats[engname][1] += 1
                if o.fn is None:
                    continue
                ins = o.fn(eng)
                stats[engname][0] += 1
                if o.dma:
                    ins.then_inc(dsem[o.dslot], 16)
                elif o.signal:
                    ins.then_inc(esem[o.eng], 1)
        with nc.Block() as block:
            @block.tensor
            def _(e):
                run("pe", e)

            @block.scalar
            def _(e):
                run("act", e)

            @block.vector
            def _(e):
                run("dve", e)

            @block.gpsimd
            def _(e):
                run("pool", e)

            @block.sync
            def _(e):
                run("sp", e)
        self.stats = stats
        return stats

import numpy as np
import ml_dtypes
from concourse.bass_utils import run_bass_kernel_spmd

D = 1024
DFF = 2816
NFC = 22
DIN = 1840
EPS = 1e-6
NLT = 32
NCT = 2
NTT = 34
S = NTT * 128
TB = 6
B_SCALE = 96 ** -0.5
C_SCALE = 64 ** -0.5
DEPTH = 2


class Ctx:
    pass


class Arena:
    def __init__(self, nc, nbytes):
        self.t = nc.alloc_sbuf_tensor("s_arena", [128, nbytes // 2], BF16)
        self.nb = nbytes
        self.off = 0

    def reset(self):
        self.off = 0

    def alloc(self, shape, dt):
        n = 1
        for v in shape[1:]:
            n *= v
        nbytes = n * (4 if dt == F32 else 2)
        nbytes = (nbytes + 31) // 32 * 32
        assert self.off + nbytes <= self.nb, ("arena overflow", self.off, nbytes, self.nb)
        v = self.t[0:shape[0], self.off // 2:(self.off + nbytes) // 2]
        self.off += nbytes
        if dt == F32:
            v = v.bitcast(F32)
        v = v[:, 0:n]
        if len(shape) == 3:
            v = v.rearrange("p (a b) -> p a b", a=shape[1])
        elif len(shape) == 4:
            v = v.rearrange("p (a b c) -> p a b c", a=shape[1], b=shape[2])
        return v


def setup_common(nc, pg):
    c = Ctx()
    c.nc, c.pg = nc, pg
    A = lambda n, sh, dt: nc.alloc_sbuf_tensor("s_" + n, sh, dt)
    c.A = A
    c.ident_b = A("ident_b", [128, 128], BF16)
    c.ident_f = A("ident_f", [128, 128], F32)
    c.triu_f = A("triu_f", [128, 128], F32)
    c.tril_f = A("tril_f", [128, 128], F32)
    c.ones_f = A("ones_f", [128, 128], F32)
    c.triu_b = A("triu_b", [128, 128], BF16)
    c.tril_b = A("tril_b", [128, 128], BF16)
    c.sel = A("sel", [2, 2, 128], F32)
    c.epsb = A("epsb", [128, 1], F32)
    c.scT = A("scT", [128, 8, 2], BF16)
    c.cT = A("cT", [128, 2, 8], F32)
    c.gT = A("gT", [128, 3, 8], F32)
    c.modT = A("modT", [128, 9, 8, 2], F32)
    c.mulT = A("mulT", [128, 3, 8, 2], F32)
    c.gate = A("gate", [128, 3, 2, D], F32)
    c.adab = [A(f"adab{i}", [2, 256], F32) for i in range(2)]
    c.modrow = [A(f"modrow{i}", [2, 256], F32) for i in range(2)]
    c.g_cq = A("g_cq", [128, 256], F32)
    c.g_ckv = A("g_ckv", [128, 128], F32)
    c.g_q = A("g_q", [128, 96], F32)
    c.g_k = A("g_k", [128, 96], F32)
    c.g_qk8 = A("g_qk8", [128, 8, 64], F32)
    c.g_on = A("g_on", [128, 256], F32)
    c.gateb = A("gateb", [128, 16], F32)
    c.esink64 = A("esink64", [64, 6], F32)
    c.ones_b = A("ones_b", [128, 128], BF16)
    c.banks = [nc.alloc_psum_tensor(f"bank{i}", [128, 512], F32) for i in range(8)]
    return c


def load_consts(c, cst):
    pg = c.pg
    for n in ("ident_b", "ident_f", "triu_f", "tril_f", "ones_f", "triu_b", "tril_b", "sel"):
        pg.dma("sp", getattr(c, n)[:], cst[n], ["d:c_" + n], [n])
    pg.op("dve", lambda e: e.memset(c.epsb[:], EPS), [], ["epsb"])
    pg.op("dve", lambda e: e.memset(c.ones_b[:], 1.0), [], ["ones_b"])


def mod_setup(c, c2_ap):
    pg = c.pg
    for r in range(2):
        pg.dma("sp", c.cT[:, r, :], c2_ap[r, :].rearrange("(k p) -> p k", p=128), ["d:c2"], ["cT"],
               allow_slow_non_contiguous=True)
    for r in range(2):
        pg.op("act", lambda e, r=r: e.activation(out=c.scT[:, :, r], in_=c.cT[:, r, :], func=AF.Silu), ["cT"], ["scT"])


def mod_layer(c, adaw_ap, adab_ap, normg_ap, wbuf, wkeys, lname):
    pg = c.pg
    for n in range(3):
        pg.dma("sp", c.gT[:, n, :], normg_ap[n, :].rearrange("(k p) -> p k", p=128), ["d:normg" + lname], ["gT"],
               allow_slow_non_contiguous=True)
    W = 256
    adab2 = adab_ap.rearrange("(o n) -> o n", o=1)
    nblk = 9 * D // W
    per_v = D // W
    nb = len(wbuf)
    for blk in range(nblk):
        wb = wbuf[blk % nb]
        wk = wkeys[blk % nb]
        pg.dma("pool", wb, adaw_ap[:, blk * W:(blk + 1) * W].rearrange("(k p) n -> p k n", p=128),
               ["d:adaw" + lname], [wk])
        ab = c.adab[blk % 2]
        abk = f"adab{blk % 2}"
        for r in range(2):
            pg.dma("sp", ab[r:r + 1, :], adab2[:, blk * W:(blk + 1) * W], ["d:adab" + lname], [abk])
        bank = c.banks[blk % 2]
        bk = f"bank{blk % 2}"
        for k in range(8):
            pg.op("pe", lambda e, k=k, wb=wb, bank=bank: e.matmul(bank[0:2, 0:W], lhsT=c.scT[:, k, :], rhs=wb[:, k, :],
                                                                  start=(k == 0), stop=(k == 7)),
                  ["scT", wk], [bk])
        mr = c.modrow[blk % 2]
        mk = f"modrow{blk % 2}"
        pg.op("dve", lambda e, mr=mr, bank=bank, ab=ab: e.tensor_tensor(out=mr[:], in0=bank[0:2, 0:W],
                                                                        in1=ab[:], op=ALU.add),
              [bk, abk], [mk])
        v, part = blk // per_v, blk % per_v
        if v in (2, 5, 8):
            gi = v // 3
            scale = 1.0 if v == 5 else 0.5
            for s in range(2):
                b2 = c.banks[2 + s]
                b2k = f"bank{2 + s}"
                pg.op("pe", lambda e, b2=b2, s=s, mr=mr: e.matmul(b2[:, 0:W], lhsT=c.sel[:, s, :], rhs=mr[:],
                                                                  start=True, stop=True), ["sel", mk], [b2k])
                pg.op("act", lambda e, b2=b2, s=s, gi=gi, part=part, scale=scale: e.activation(
                    out=c.gate[:, gi, s, part * W:(part + 1) * W], in_=b2[:, 0:W], func=AF.Copy, scale=scale),
                    [b2k], ["gate"])
        else:
            b2 = c.banks[4 + (blk % 2)]
            b2k = f"bank{4 + (blk % 2)}"
            nq = W // 128
            for q in range(nq):
                pg.op("pe", lambda e, b2=b2, q=q, mr=mr: e.transpose(b2[:, 2 * q:2 * q + 2], mr[0:2, q * 128:(q + 1) * 128],
                                                                     c.ident_f[0:2, 0:2]), [mk, "ident_f"], [b2k])
            pg.op("dve", lambda e, b2=b2, v=v, part=part, nq=nq: e.tensor_copy(
                out=c.modT[:, v, part * nq:(part + 1) * nq, :], in_=b2[:, 0:2 * nq].rearrange("p (q s) -> p q s", s=2)),
                [b2k], ["modT"])
    for n in range(3):
        for s in range(2):
            pg.op("dve", lambda e, n=n, s=s: e.scalar_tensor_tensor(out=c.mulT[:, n, :, s], in0=c.modT[:, 1 + 3 * n, :, s],
                                                                    scalar=1.0, in1=c.gT[:, n, :], op0=ALU.add, op1=ALU.mult),
                  ["modT", "gT"], ["mulT"])


def ffn_layout(c):
    ar = c.ar
    ar.reset()
    b = {}
    b["x"] = ar.alloc([128, TB, D], F32)
    b["hT"] = ar.alloc([128, 8, TB * 128], BF16)
    b["actT"] = ar.alloc([128, NFC, TB * 128], BF16)
    b["wg"] = [ar.alloc([128, 8, 256], BF16) for i in range(3)]
    b["wu"] = [ar.alloc([128, 8, 256], BF16) for i in range(3)]
    b["wo"] = [ar.alloc([128, 4, 512], BF16) for i in range(3)]
    b["sg"] = [ar.alloc([128, TB * 128], BF16) for i in range(2)]
    b["tmp"] = [ar.alloc([128, 512], F32) for i in range(2)]
    b["junk"] = ar.alloc([128, D], BF16)
    b["xn"] = [ar.alloc([128, D], BF16) for i in range(2)]
    b["ssall"] = ar.alloc([128, 8], F32)
    b["ss"] = [ar.alloc([128, 1], F32) for i in range(2)]
    b["rstd"] = [ar.alloc([128, 1], F32) for i in range(2)]
    b["pst"] = [ar.alloc([128, DIN], F32) for i in range(2)]
    b["yT"] = ar.alloc([128, 8, TB * 128], BF16)
    b["wout"] = ar.alloc([128, 8, D], BF16)
    b["wi_cnt"] = 0
    b["wo_cnt"] = 0
    return b


def norm_mod_T(c, b, slots, isctx, n, tbank_base=6):
    pg = c.pg
    hT = b["hT"]
    x = b["x"]
    nt = len(slots)
    ssall = b["ssall"]
    for j, sl in enumerate(slots):
        pg.op("act", lambda e, sl=sl, j=j: e.activation(out=b["junk"], in_=x[:, sl, :], func=AF.Square, accum_out=ssall[:, j:j + 1]),
              [f"xb{sl}"], ["junk", f"ssall{j}"])
    SK = [f"ssall{j}" for j in range(nt)]
    pg.op("act", lambda e: e.activation(out=ssall[:, 0:nt], in_=ssall[:, 0:nt], func=AF.Sqrt, scale=1.0 / D, bias=c.epsb[:]),
          SK + ["epsb"], SK)
    pg.op("dve", lambda e: e.reciprocal(out=ssall[:, 0:nt], in_=ssall[:, 0:nt]), SK, SK)
    for j, sl in enumerate(slots):
        i = j % 2
        s = 1 if isctx[j] else 0
        pg.op("act", lambda e, sl=sl, i=i, j=j: e.activation(out=b["xn"][i], in_=x[:, sl, :], func=AF.Copy, scale=ssall[:, j:j + 1]),
              [f"xb{sl}", f"ssall{j}"], [f"xn{i}"])
        bank = c.banks[tbank_base + i]
        bk = f"bank{tbank_base + i}"
        bb = bank[:, :].bitcast(BF16)
        for k in range(8):
            pg.op("pe", lambda e, k=k, i=i, bb=bb: e.transpose(bb[:, k * 128:(k + 1) * 128], b["xn"][i][:, k * 128:(k + 1) * 128],
                                                               c.ident_b[:, :]), [f"xn{i}", "ident_b"], [bk])
        for k in range(8):
            if k % 2 == 0:
                pg.op("act", lambda e, k=k, bb=bb, j=j, n=n, s=s: e.activation(
                    out=hT[:, k, j * 128:(j + 1) * 128], in_=bb[:, k * 128:(k + 1) * 128], func=AF.Identity,
                    scale=c.mulT[:, n, k, s:s + 1], bias=c.modT[:, 3 * n, k, s:s + 1]),
                    [bk, "mulT", "modT"], [f"hT{j}"])
            else:
                pg.op("dve", lambda e, k=k, bb=bb, j=j, n=n, s=s: e.tensor_scalar(
                    out=hT[:, k, j * 128:(j + 1) * 128], in0=bb[:, k * 128:(k + 1) * 128],
                    scalar1=c.mulT[:, n, k, s:s + 1], scalar2=c.modT[:, 3 * n, k, s:s + 1], op0=ALU.mult, op1=ALU.add),
                    [bk, "mulT", "modT"], [f"hT{j}"])


def ffn_block(c, b, slots, isctx, n, gi, wi_ap, wo_ap, lname, cached=False):
    pg = c.pg
    nt = len(slots)
    NTOK = nt * 128
    hT, actT = b["hT"], b["actT"]
    x = b["x"]
    norm_mod_T(c, b, slots, isctx, n)
    hkeys = [f"hT{j}" for j in range(nt)]
    halves = []
    off = 0
    while off < NTOK:
        w = min(512, NTOK - off)
        halves.append((off, w))
        off += w
    for jb in range(11):
        st = b["wi_cnt"] % 3
        b["wi_cnt"] += 1
        wg, wu = b["wg"][st], b["wu"][st]
        if cached:
            pg.dma("sp", wg, wi_ap[(jb * 2) * 128:(jb * 2 + 1) * 128, :].rearrange("p (k n) -> p k n", k=8),
                   ["d:wi" + lname], [f"wg{st}"])
            pg.dma("sp", wu, wi_ap[(jb * 2 + 1) * 128:(jb * 2 + 2) * 128, :].rearrange("p (k n) -> p k n", k=8),
                   ["d:wi" + lname], [f"wu{st}"])
        else:
            pg.dma("pool", wg, wi_ap[:, jb * 256:(jb + 1) * 256].rearrange("(k p) n -> p k n", p=128),
                   ["d:wi" + lname], [f"wg{st}"])
            pg.dma("pool", wu, wi_ap[:, DFF + jb * 256:DFF + (jb + 1) * 256].rearrange("(k p) n -> p k n", p=128),
                   ["d:wi" + lname], [f"wu{st}"])
        for m in range(2):
            ch = jb * 2 + m
            p = ch % 2
            gbanks = [0 + 2 * p, 1 + 2 * p]
            ub = [4, 5]
            for k in range(8):
                for hi, (o0, w) in enumerate(halves):
                    pg.op("pe", lambda e, k=k, hi=hi, o0=o0, w=w, wg=wg, m=m, gbanks=gbanks: e.matmul(
                        c.banks[gbanks[hi]][:, 0:w], lhsT=wg[:, k, m * 128:(m + 1) * 128], rhs=hT[:, k, o0:o0 + w],
                        start=(k == 0), stop=(k == 7)), [f"wg{st}"] + hkeys, [f"bank{gbanks[hi]}"])
            for k in range(8):
                for hi, (o0, w) in enumerate(halves):
                    pg.op("pe", lambda e, k=k, hi=hi, o0=o0, w=w, wu=wu, m=m, ub=ub: e.matmul(
                        c.banks[ub[hi]][:, 0:w], lhsT=wu[:, k, m * 128:(m + 1) * 128], rhs=hT[:, k, o0:o0 + w],
                        start=(k == 0), stop=(k == 7)), [f"wu{st}"] + hkeys, [f"bank{ub[hi]}"])
            sg = b["sg"][p]
            for hi, (o0, w) in enumerate(halves):
                pg.op("act", lambda e, hi=hi, o0=o0, w=w, sg=sg, gbanks=gbanks: e.activation(
                    out=sg[:, o0:o0 + w], in_=c.banks[gbanks[hi]][:, 0:w], func=AF.Silu),
                    [f"bank{gbanks[hi]}"], [f"sg{p}_{hi}"])
                pg.op("dve", lambda e, hi=hi, o0=o0, w=w, sg=sg, ub=ub, ch=ch: e.tensor_tensor(
                    out=actT[:, ch, o0:o0 + w], in0=c.banks[ub[hi]][:, 0:w], in1=sg[:, o0:o0 + w], op=ALU.mult),
                    [f"bank{ub[hi]}", f"sg{p}_{hi}"], [f"actT{ch}"])
    for oh in range(2):
        for sg_ in range(6):
            st = b["wo_cnt"] % 3
            b["wo_cnt"] += 1
            wo = b["wo"][st]
            ng = 2 if sg_ < 5 else 1
            cg0 = sg_ * 2
            if cached:
                r0 = (oh * 11 + cg0) * 128
                pg.dma("sp", wo.rearrange("p (g c) n -> p g (c n)", c=2)[:, 0:ng, :],
                       wo_ap[r0:r0 + ng * 128, :].rearrange("(g p) m -> p g m", p=128),
                       ["d:wo" + lname], [f"wo{st}"])
            else:
                pg.dma("pool", wo[:, 0:2 * ng, :],
                       wo_ap[cg0 * 256:(cg0 + ng) * 256, oh * 512:(oh + 1) * 512].rearrange("(c p) n -> p c n", p=128),
                       ["d:wo" + lname], [f"wo{st}"])
            for cc in range(2 * ng):
                ch = cg0 * 2 + cc
                for j in range(nt):
                    pg.op("pe", lambda e, j=j, ch=ch, cc=cc, wo=wo: e.matmul(
                        c.banks[j][:, :], lhsT=actT[:, ch, j * 128:(j + 1) * 128], rhs=wo[:, cc, :],
                        start=(ch == 0), stop=(ch == NFC - 1)), [f"actT{ch}", f"wo{st}"], [f"bank{j}"])
        for j, sl in enumerate(slots):
            s = 1 if isctx[j] else 0
            tmp = b["tmp"][j % 2]
            pg.op("dve", lambda e, j=j, s=s, tmp=tmp, oh=oh: e.tensor_tensor(
                out=tmp, in0=c.banks[j][:, :], in1=c.gate[:, gi, s, oh * 512:(oh + 1) * 512], op=ALU.mult),
                [f"bank{j}", "gate"], [f"tmp{j % 2}"])
            pg.op("pool", lambda e, sl=sl, tmp=tmp, oh=oh: e.tensor_tensor(
                out=x[:, sl, oh * 512:(oh + 1) * 512], in0=x[:, sl, oh * 512:(oh + 1) * 512], in1=tmp, op=ALU.add),
                [f"tmp{j % 2}", f"xb{sl}"], [f"xb{sl}"])


ACTK = [f"actT{ch}" for ch in range(NFC)]


def win_block(c, b, tiles, isctx, win_ap, ps_ap, lname):
    pg = c.pg
    nt = len(tiles)
    wv = b["actT"].rearrange("p a b -> p (a b)")[:, 0:8 * DIN].rearrange("p (k n) -> p k n", k=8)
    pg.dma("pool", wv, win_ap.rearrange("(k p) n -> p k n", p=128), ["d:win" + lname], ACTK)
    norm_mod_T(c, b, list(range(nt)), isctx, 1)
    cols = [(0, 512), (512, 512), (1024, 512), (1536, DIN - 1536)]
    for j, t in enumerate(tiles):
        pst = b["pst"][j % 2]
        for ci, (c0, w) in enumerate(cols):
            bank = c.banks[ci + 4 * (j % 2)] if ci + 4 * (j % 2) < 6 else c.banks[ci - 2 + 4 * (j % 2) - 4]
            bi = (ci + 4 * (j % 2)) % 6
            bank = c.banks[bi]
            for k in range(8):
                pg.op("pe", lambda e, k=k, j=j, c0=c0, w=w, bank=bank: e.matmul(
                    bank[:, 0:w], lhsT=b["hT"][:, k, j * 128:(j + 1) * 128], rhs=wv[:, k, c0:c0 + w],
                    start=(k == 0), stop=(k == 7)), ACTK + [f"hT{j}"], [f"bank{bi}"])
            pg.op("act" if ci % 2 == 0 else "dve",
                  (lambda e, pst=pst, c0=c0, w=w, bank=bank: e.activation(out=pst[:, c0:c0 + w], in_=bank[:, 0:w], func=AF.Copy))
                  if ci % 2 == 0 else
                  (lambda e, pst=pst, c0=c0, w=w, bank=bank: e.tensor_copy(out=pst[:, c0:c0 + w], in_=bank[:, 0:w])),
                  [f"bank{bi}"], [f"pst{j % 2}_{ci}"])
        pg.dma("sp", ps_ap[t * 128:(t + 1) * 128, :], pst, [f"pst{j % 2}_{ci}" for ci in range(4)], [f"d:ps{t}"])


def wout_block(c, b, tiles, isctx, YT_ap, lname):
    pg = c.pg
    nt = len(tiles)
    yT = b["yT"]
    t0 = tiles[0]
    pg.dma("sp", yT[:, :, 0:nt * 128], YT_ap[:, t0 * 128:(t0 + nt) * 128].rearrange("(k p) t -> p k t", p=128),
           ["d:YT"], ["yTb"])
    x = b["x"]
    for j in range(nt):
        s = 1 if isctx[j] else 0
        for oh in range(2):
            bi = (2 * j + oh) % 6
            bank = c.banks[bi]
            for k in range(8):
                pg.op("pe", lambda e, k=k, j=j, oh=oh, bank=bank: e.matmul(
                    bank[:, :], lhsT=yT[:, k, j * 128:(j + 1) * 128], rhs=b["wout"][:, k, oh * 512:(oh + 1) * 512],
                    start=(k == 0), stop=(k == 7)), ["yTb", "wout"], [f"bank{bi}"])
            tmp = b["tmp"][oh]
            pg.op("dve", lambda e, s=s, tmp=tmp, oh=oh, bank=bank: e.tensor_tensor(
                out=tmp, in0=bank[:, :], in1=c.gate[:, 1, s, oh * 512:(oh + 1) * 512], op=ALU.mult),
                [f"bank{bi}", "gate"], [f"tmp{oh}"])
            pg.op("pool", lambda e, j=j, tmp=tmp, oh=oh: e.tensor_tensor(
                out=x[:, j, oh * 512:(oh + 1) * 512], in0=x[:, j, oh * 512:(oh + 1) * 512], in1=tmp, op=ALU.add),
                [f"tmp{oh}", f"xb{j}"], [f"xb{j}"])


def cast_weights_wi(c, src_ap, dst_ap, key):
    pg = c.pg
    for jb in range(11):
        for gu in range(2):
            c0 = gu * DFF + jb * 256
            r0 = (jb * 2 + gu) * 128
            pg.dma("pool", dst_ap[r0:r0 + 128, :].rearrange("p (k n) -> p k n", k=8),
                   src_ap[:, c0:c0 + 256].rearrange("(k p) n -> p k n", p=128), [], ["d:" + key])


def cast_weights_wo(c, src_ap, dst_ap, key):
    pg = c.pg
    for oh in range(2):
        for cg in range(11):
            r0 = (oh * 11 + cg) * 128
            pg.dma("pool", dst_ap[r0:r0 + 128, :].rearrange("p (c n) -> p c n", c=2),
                   src_ap[cg * 256:(cg + 1) * 256, oh * 512:(oh + 1) * 512].rearrange("(c p) n -> p c n", p=128), [], ["d:" + key])

def mixer_layout(c):
    ar = c.ar
    ar.reset()
    m = {}
    m["Vr"] = ar.alloc([128, NTT, 6, 65], BF16)
    m["GV"] = ar.alloc([128, NTT, 2, 65], BF16)
    m["GKT"] = ar.alloc([64, 2, S], BF16)
    m["pt"] = [ar.alloc([128, DIN], F32) for i in range(2)]
    m["rt"] = [ar.alloc([128, 192], F32) for i in range(2)]
    m["junk"] = ar.alloc([128, 256], F32)
    m["s1"] = ar.alloc([128, 8], F32)
    m["s6"] = ar.alloc([128, 8], F32)
    m["s8"] = ar.alloc([128, 8], F32)
    m["s4"] = ar.alloc([128, 8], F32)
    m["s9"] = ar.alloc([128, 8], F32)
    m["sq"] = ar.alloc([128, 576], F32)
    m["ckvn"] = ar.alloc([128, 128], BF16)
    m["ckvnT"] = ar.alloc([128, 128], BF16)
    m["kvs"] = ar.alloc([128, 6, 128], F32)
    m["tmp6"] = ar.alloc([128, 6, 64], F32)
    m["krg"] = ar.alloc([128, 32], F32)
    m["krr"] = ar.alloc([128, 32], F32)
    m["krt"] = ar.alloc([128, 32], F32)
    m["krc"] = ar.alloc([128, 32], F32)
    m["kf"] = ar.alloc([128, 6, 96], BF16)
    m["ktT"] = ar.alloc([96, 6, 128], BF16)
    m["cqn"] = ar.alloc([128, 256], BF16)
    m["cqnT"] = ar.alloc([128, 2, 128], BF16)
    m["qs"] = ar.alloc([128, 6, 96], F32)
    m["qn"] = ar.alloc([128, 6, 96], F32)
    m["tr"] = ar.alloc([128, 6, 32], F32)
    m["tc"] = ar.alloc([128, 6, 32], F32)
    m["qf"] = ar.alloc([128, 6, 96], BF16)
    m["qtT"] = ar.alloc([96, 6, 128], BF16)
    m["n8"] = ar.alloc([128, 8, 64], F32)
    m["t8"] = ar.alloc([128, 8, 64], F32)
    m["c8"] = ar.alloc([128, 8, 64], F32)
    m["g8"] = ar.alloc([128, 8, 64], BF16)
    m["gqT"] = ar.alloc([64, 6, 128], BF16)
    m["mqk"] = ar.alloc([128, 256], BF16)
    m["mT"] = ar.alloc([32, 8, 128], BF16)
    m["gt"] = ar.alloc([128, 16], F32)
    m["ge"] = ar.alloc([128, 2, 4], F32)
    m["wukv"] = ar.alloc([128, 768], BF16)
    m["wuq"] = ar.alloc([128, 2, 576], BF16)
    m["scan"] = []
    for d in range(2):
        sb = {}
        sb["mTi"] = [ar.alloc([32, 8, 128], BF16) for i in range(2)]
        sb["ktok"] = [ar.alloc([128, 128], BF16) for i in range(2)]
        sb["gti"] = [ar.alloc([128, 16], F32) for i in range(2)]
        sb["v"] = [ar.alloc([128, 256], F32) for i in range(2)]
        sb["rF"] = ar.alloc([128, 8], F32)
        sb["du"] = ar.alloc([128, 4], F32)
        sb["u"] = ar.alloc([128, 4], F32)
        sb["vpp"] = ar.alloc([128, 4, 65], BF16)
        sb["PTm"] = ar.alloc([128, 4, 128], BF16)
        sb["C"] = ar.alloc([32, 4, 65], F32)
        sb["Cb"] = ar.alloc([32, 4, 65], BF16)
        sb["t1"] = ar.alloc([128, 4, 65], F32)
        sb["den"] = ar.alloc([128, 4], F32)
        sb["dneg"] = ar.alloc([128, 4], F32)
        sb["hd"] = ar.alloc([128, 4, 64], F32)
        m["scan"].append(sb)
    m["hfi"] = [m["pt"][0][:, 256 * i:256 * (i + 1)] for i in range(2)]
    m["hbi"] = [m["pt"][0][:, 512 + 256 * i:512 + 256 * (i + 1)] for i in range(2)]
    m["oi"] = [m["pt"][0][:, 1024 + 256 * i:1024 + 256 * (i + 1)] for i in range(2)]
    m["hs"] = ar.alloc([128, 4, 64], F32)
    m["hq"] = ar.alloc([128, 4, 64], F32)
    m["sgo"] = ar.alloc([128, 256], F32)
    m["ya"] = ar.alloc([128, 256], BF16)
    m["yaT"] = ar.alloc([128, 2, 128], BF16)
    m["KTh"] = [ar.alloc([96, S], BF16) for i in range(1)]
    m["QTh"] = [ar.alloc([96, S], BF16) for i in range(1)]
    m["PT"] = [ar.alloc([128, 512], BF16) for i in range(4)]
    m["rden"] = [ar.alloc([64, 512], F32) for i in range(2)]
    m["accs"] = [ar.alloc([65, 512], F32) for i in range(2)]
    m["yh"] = [ar.alloc([64, 512], BF16) for i in range(2)]
    m["q3"] = [ar.alloc([64, 3, 128], BF16) for i in range(4)]
    return m


def load_layer_small(c, m, W, l):
    pg = c.pg

    def bc(dst, src, key):
        pg.dma("sp", dst, src.partition_broadcast(128), ["d:" + key], [key])
    bc(c.g_cq[:], W["mla_cq_norm"][l], "g_cq")
    bc(c.g_ckv[:], W["mla_ckv_norm"][l], "g_ckv")
    bc(c.g_q[:], W["mla_q_norm"][l], "g_q")
    bc(c.g_k[:], W["mla_k_norm"][l], "g_k")
    for h in range(6):
        bc(c.g_qk8[:, h, :], W["gqa_q_norm"][l], "g_qk8")
    for h in range(6, 8):
        bc(c.g_qk8[:, h, :], W["gqa_k_norm"][l], "g_qk8")
    bc(c.g_on[:], W["mlstm_out_norm"][l], "g_on")
    bc(c.gateb[:], W["mlstm_gate_b"][l], "gateb")
    pg.dma("sp", c.esink64[:], W["gqa_sink"][l].partition_broadcast(64), ["d:sink"], ["esink64"])
    pg.op("act", lambda e: e.activation(out=c.esink64[:], in_=c.esink64[:], func=AF.Exp), ["esink64"], ["esink64"])
    pg.dma("pool", m["wukv"], W["mla_w_ukv"][l], ["d:wukv"], ["wukv"])
    pg.dma("pool", m["wuq"], W["mla_w_uq"][l].rearrange("(k p) n -> p k n", p=128), ["d:wuq"], ["wuq"])
    pg.op("dve", lambda e: e.memset(m["Vr"][:, :, :, 64:65], 1.0), [], ["Vr1"])
    pg.op("dve", lambda e: e.memset(m["GV"][:, :, :, 64:65], 1.0), [], ["GV1"])


def rstd_multi(c, m, src, H, d, sq_tmp, ss, key_src, key_out, extra=None, sqkey="sqtmp"):
    pg = c.pg
    pg.op("act", lambda e: e.activation(out=sq_tmp, in_=src, func=AF.Square), key_src, [sqkey])
    pg.op("dve", lambda e: e.tensor_reduce(out=ss[:, 0:H], in_=sq_tmp, axis=AX.X, op=ALU.add), [sqkey], [key_out])
    if extra is not None:
        ex, exk = extra
        pg.op("dve", lambda e: e.tensor_scalar(out=ss[:, 0:H], in0=ss[:, 0:H], scalar1=ex, scalar2=None, op0=ALU.add),
              [key_out, exk], [key_out])
    pg.op("act", lambda e: e.activation(out=ss[:, 0:H], in_=ss[:, 0:H], func=AF.Ln, scale=1.0 / d, bias=c.epsb[:]),
          [key_out, "epsb"], [key_out])
    pg.op("act", lambda e: e.activation(out=ss[:, 0:H], in_=ss[:, 0:H], func=AF.Exp, scale=-0.5), [key_out], [key_out])


def rstd_single(c, m, src, d, ss, key_src, key_out, scale_d=None):
    pg = c.pg
    pg.op("act", lambda e: e.activation(out=m["junk"][:, 0:d], in_=src, func=AF.Square, accum_out=ss), key_src, ["junk", key_out])
    pg.op("act", lambda e: e.activation(out=ss, in_=ss, func=AF.Ln, scale=1.0 / d, bias=c.epsb[:]),
          [key_out, "epsb"], [key_out])
    pg.op("act", lambda e: e.activation(out=ss, in_=ss, func=AF.Exp, scale=-0.5), [key_out], [key_out])


def rope_multi(c, src, dst, tmp, tmpc, cos, sins, H, R, ksrc, kdst, ktab, kt="ropetmp", kc="ropetmpc"):
    pg = c.pg
    Q = R // 4
    for q in range(4):
        q2 = q ^ 1
        pg.op("dve", lambda e, q=q, q2=q2: e.tensor_tensor(
            out=tmp[:, :, q * Q:(q + 1) * Q], in0=src[:, :, q2 * Q:(q2 + 1) * Q],
            in1=sins[:, None, q * Q:(q + 1) * Q].broadcast_to([128, H, Q]), op=ALU.mult), ksrc + [ktab], [kt])
    pg.op("dve", lambda e: e.tensor_tensor(out=tmpc, in0=src, in1=cos[:, None, :].broadcast_to([128, H, R]), op=ALU.mult),
          ksrc + [ktab], [kc])
    pg.op("dve", lambda e: e.tensor_tensor(out=dst, in0=tmpc, in1=tmp, op=ALU.add), [kt, kc], kdst)


def prep_tile(c, m, t, D_, need_ctx):
    pg = c.pg
    isctx = t >= NLT
    i = t % 2
    pt = m["pt"][i]
    ptk = f"pt{i}"
    rt = m["rt"][i]
    rtk = f"rt{i}"
    pg.begin_capture()
    pg.dma("sp", pt, D_["ps"][t * 128:(t + 1) * 128, :], [f"d:ps{t}"], [ptk])
    if not isctx:
        pg.dma("sp", rt, D_["rope"][t * 128:(t + 1) * 128, :], ["d:rope"], [rtk])
    TB_ = c.banks[7][:, :].bitcast(BF16)
    pg.mark()
    s1 = m["s1"]
    rstd_single(c, m, pt[:, 1040:1168], 128, s1[:, 0:1], [ptk], "s1a")
    pg.op("dve", lambda e: e.scalar_tensor_tensor(out=m["ckvn"], in0=pt[:, 1040:1168], scalar=s1[:, 0:1], in1=c.g_ckv[:],
                                                  op0=ALU.mult, op1=ALU.mult), [ptk, "s1a", "g_ckv"], ["ckvn"])
    pg.op("pe", lambda e: e.transpose(TB_[:, 0:128], m["ckvn"], c.ident_b[:, :]), ["ckvn", "ident_b"], ["bank7"])
    pg.op("act", lambda e: e.activation(out=m["ckvnT"], in_=TB_[:, 0:128], func=AF.Copy), ["bank7"], ["ckvnT"])
    pg.op("pe", lambda e: e.matmul(c.banks[0][:, 0:512], lhsT=m["ckvnT"], rhs=m["wukv"][:, 0:512], start=True, stop=True),
          ["ckvnT", "wukv"], ["bank0"])
    pg.op("pe", lambda e: e.matmul(c.banks[1][:, 0:256], lhsT=m["ckvnT"], rhs=m["wukv"][:, 512:768], start=True, stop=True),
          ["ckvnT", "wukv"], ["bank1"])
    kvs = m["kvs"]
    pg.op("act", lambda e: e.activation(out=kvs[:, 0:4, :], in_=c.banks[0][:, 0:512].rearrange("p (h j) -> p h j", h=4), func=AF.Copy),
          ["bank0"], ["kvsA"])
    pg.op("act", lambda e: e.activation(out=kvs[:, 4:6, :], in_=c.banks[1][:, 0:256].rearrange("p (h j) -> p h j", h=2), func=AF.Copy),
          ["bank1"], ["kvsB"])
    KV = ["kvsA", "kvsB"]
    pg.op("dve", lambda e: e.tensor_copy(out=m["Vr"][:, t, :, 0:64], in_=kvs[:, :, 64:128]), KV, [f"Vr{t}"])
    pg.op("act", lambda e: e.activation(out=m["junk"][:, 0:32], in_=pt[:, 1168:1200], func=AF.Square, accum_out=s1[:, 1:2]),
          [ptk], ["junk", "s1b"])
    rstd_multi(c, m, kvs[:, :, 0:64], 6, 96, m["tmp6"], m["s6"], KV, "s6", extra=(s1[:, 1:2], "s1b"), sqkey="tmp6b")
    rk = m["s6"]
    pg.op("dve", lambda e: e.tensor_tensor(out=m["tmp6"], in0=kvs[:, :, 0:64], in1=rk[:, 0:6].unsqueeze(2).broadcast_to([128, 6, 64]),
                                           op=ALU.mult), KV + ["s6"], ["tmp6b"])
    pg.op("dve", lambda e: e.tensor_tensor(out=m["kf"][:, :, 0:64], in0=m["tmp6"], in1=c.g_k[:, None, 0:64].broadcast_to([128, 6, 64]),
                                           op=ALU.mult), ["tmp6b", "g_k"], ["kfA"])
    pg.op("dve", lambda e: e.tensor_tensor(out=m["krg"], in0=pt[:, 1168:1200], in1=c.g_k[:, 64:96], op=ALU.mult), [ptk, "g_k"], ["krg"])
    if not isctx:
        rope_multi(c, m["krg"].unsqueeze(1), m["krr"].unsqueeze(1), m["krt"].unsqueeze(1), m["krc"].unsqueeze(1),
                   rt[:, 0:32], rt[:, 32:64], 1, 32, ["krg"], ["krr"], rtk, kt="ropetmpK", kc="ropetmpcK")
        krr, krrk = m["krr"], "krr"
    else:
        krr, krrk = m["krg"], "krg"
    pg.op("dve", lambda e: e.tensor_tensor(out=m["kf"][:, :, 64:96], in0=krr[:, None, :].broadcast_to([128, 6, 32]),
                                           in1=rk[:, 0:6].unsqueeze(2).broadcast_to([128, 6, 32]), op=ALU.mult),
          [krrk, "s6"], ["kfB"])
    for h in range(6):
        pg.op("pe", lambda e, h=h: e.transpose(TB_[0:96, h * 128:(h + 1) * 128], m["kf"][:, h, :], c.ident_b[:, :]),
              ["kfA", "kfB", "ident_b"], ["bank7"])
    pg.op("act", lambda e: e.activation(out=m["ktT"], in_=TB_[0:96, 0:768].rearrange("p (h t) -> p h t", h=6), func=AF.Copy),
          ["bank7"], ["ktT"])
    pg.dma("sp", D_["KT"][:, :, t * 128:(t + 1) * 128].rearrange("h d t -> d h t"), m["ktT"], ["ktT"], ["d:KT"])
    pg.mark()
    if (not isctx) or need_ctx:
        rstd_single(c, m, pt[:, 784:1040], 256, s1[:, 2:3], [ptk], "s1c")
        pg.op("dve", lambda e: e.scalar_tensor_tensor(out=m["cqn"], in0=pt[:, 784:1040], scalar=s1[:, 2:3], in1=c.g_cq[:],
                                                      op0=ALU.mult, op1=ALU.mult), [ptk, "s1c", "g_cq"], ["cqn"])
        TB6 = c.banks[6][:, :].bitcast(BF16)
        for kk in range(2):
            pg.op("pe", lambda e, kk=kk: e.transpose(TB6[:, kk * 128:(kk + 1) * 128], m["cqn"][:, kk * 128:(kk + 1) * 128], c.ident_b[:, :]),
                  ["cqn", "ident_b"], ["bank6"])
        pg.op("act", lambda e: e.activation(out=m["cqnT"], in_=TB6[:, 0:256].rearrange("p (k t) -> p k t", k=2), func=AF.Copy),
              ["bank6"], ["cqnT"])
        for kk in range(2):
            pg.op("pe", lambda e, kk=kk: e.matmul(c.banks[2][:, 0:512], lhsT=m["cqnT"][:, kk, :], rhs=m["wuq"][:, kk, 0:512],
                                                  start=(kk == 0), stop=(kk == 1)), ["cqnT", "wuq"], ["bank2"])
        for kk in range(2):
            pg.op("pe", lambda e, kk=kk: e.matmul(c.banks[3][:, 0:64], lhsT=m["cqnT"][:, kk, :], rhs=m["wuq"][:, kk, 512:576],
                                                  start=(kk == 0), stop=(kk == 1)), ["cqnT", "wuq"], ["bank3"])
        qsf = m["qs"].rearrange("p h j -> p (h j)")
        pg.op("act", lambda e: e.activation(out=qsf[:, 0:512], in_=c.banks[2][:, 0:512], func=AF.Copy), ["bank2"], ["qsA"])
        pg.op("act", lambda e: e.activation(out=qsf[:, 512:576], in_=c.banks[3][:, 0:64], func=AF.Copy), ["bank3"], ["qsB"])
        QS = ["qsA", "qsB"]
        rstd_multi(c, m, m["qs"], 6, 96, m["sq"][:, 0:576].rearrange("p (h j) -> p h j", h=6), m["s8"], QS, "s8q")
        rq = m["s8"]
        pg.op("dve", lambda e: e.tensor_tensor(out=m["qn"], in0=m["qs"], in1=rq[:, 0:6].unsqueeze(2).broadcast_to([128, 6, 96]),
                                               op=ALU.mult), QS + ["s8q", "sqtmp"], ["qn"])
        pg.op("dve", lambda e: e.tensor_tensor(out=m["qn"], in0=m["qn"], in1=c.g_q[:, None, :].broadcast_to([128, 6, 96]),
                                               op=ALU.mult), ["qn", "g_q"], ["qn"])
        if not isctx:
            pg.op("dve", lambda e: e.tensor_copy(out=m["qf"][:, :, 0:64], in_=m["qn"][:, :, 0:64]), ["qn"], ["qfA"])
            rope_multi(c, m["qn"][:, :, 64:96], m["qf"][:, :, 64:96], m["tr"], m["tc"], rt[:, 0:32], rt[:, 32:64], 6, 32,
                       ["qn"], ["qfB"], rtk)
        else:
            pg.op("dve", lambda e: e.tensor_copy(out=m["qf"], in_=m["qn"]), ["qn"], ["qfA", "qfB"])
        for h in range(6):
            pg.op("pe", lambda e, h=h: e.transpose(TB6[0:96, h * 128:(h + 1) * 128], m["qf"][:, h, :], c.ident_b[:, :]),
                  ["qfA", "qfB", "ident_b"], ["bank6"])
        pg.op("act", lambda e: e.activation(out=m["qtT"], in_=TB6[0:96, 0:768].rearrange("p (h t) -> p h t", h=6), func=AF.Copy),
              ["bank6"], ["qtT"])
        pg.dma("sp", D_["QT"][:, :, t * 128:(t + 1) * 128].rearrange("h d t -> d h t"), m["qtT"], ["qtT"], ["d:QT"])
    pg.mark()
    qk = pt[:, 1200:1712].rearrange("p (h j) -> p h j", h=8)
    rstd_multi(c, m, qk, 8, 64, m["t8"], m["s9"], [ptk], "s8g", sqkey="ropetmp8")
    pg.op("dve", lambda e: e.tensor_tensor(out=m["n8"], in0=qk, in1=m["s9"][:, 0:8].unsqueeze(2).broadcast_to([128, 8, 64]),
                                           op=ALU.mult), [ptk, "s8g", "ropetmp8"], ["n8"])
    pg.op("dve", lambda e: e.tensor_tensor(out=m["n8"], in0=m["n8"], in1=c.g_qk8[:], op=ALU.mult), ["n8", "g_qk8"], ["n8"])
    if not isctx:
        rope_multi(c, m["n8"], m["g8"], m["t8"], m["c8"], rt[:, 64:128], rt[:, 128:192], 8, 64, ["n8"], ["g8"], rtk,
                   kt="ropetmp8", kc="ropetmpc8")
    else:
        pg.op("dve", lambda e: e.tensor_copy(out=m["g8"], in_=m["n8"]), ["n8"], ["g8"])
    TB5 = c.banks[5][:, :].bitcast(BF16)
    for h in range(8):
        pg.op("pe", lambda e, h=h: e.transpose(TB5[0:64, h * 128:(h + 1) * 128], m["g8"][:, h, :], c.ident_b[:, :]),
              ["g8", "ident_b"], ["bank5"])
    pg.op("act", lambda e: e.activation(out=m["gqT"], in_=TB5[0:64, 0:768].rearrange("p (h t) -> p h t", h=6), func=AF.Copy),
          ["bank5"], ["gqT"])
    pg.op("act", lambda e: e.activation(out=m["GKT"][:, :, t * 128:(t + 1) * 128],
                                        in_=TB5[0:64, 768:1024].rearrange("p (h t) -> p h t", h=2), func=AF.Copy),
          ["bank5"], [f"GKT{t}"])
    pg.dma("sp", D_["GQT"][:, :, t * 128:(t + 1) * 128].rearrange("h d t -> d h t"), m["gqT"], ["gqT"], ["d:GQT"])
    pg.op("dve", lambda e: e.tensor_copy(out=m["GV"][:, t, :, 0:64], in_=pt[:, 1712:1840].rearrange("p (h j) -> p h j", h=2)),
          [ptk], [f"GV{t}"])
    pg.mark()
    pg.op("act", lambda e: e.activation(out=m["mqk"][:, 0:128], in_=pt[:, 0:128], func=AF.Copy, scale=32 ** -0.5), [ptk], ["mqkA"])
    pg.op("dve", lambda e: e.tensor_copy(out=m["mqk"][:, 128:256], in_=pt[:, 128:256]), [ptk], ["mqkB"])
    TB4 = c.banks[4][:, :].bitcast(BF16)
    for j in range(8):
        pg.op("pe", lambda e, j=j: e.transpose(TB4[0:32, j * 128:(j + 1) * 128], m["mqk"][:, j * 32:(j + 1) * 32], c.ident_b[:, :]),
              ["mqkA", "mqkB", "ident_b"], ["bank4"])
    pg.op("act", lambda e: e.activation(out=m["mT"], in_=TB4[0:32, 0:1024].rearrange("p (j t) -> p j t", j=8), func=AF.Copy),
          ["bank4"], ["mT"])
    pg.dma("sp", D_["MQKT"][:, :, t * 128:(t + 1) * 128].rearrange("j d t -> d j t"), m["mT"], ["mT"], ["d:MQKT"])
    pg.dma("sp", D_["MKtok"][t * 128:(t + 1) * 128, :], m["mqk"][:, 128:256], ["mqkB"], ["d:MKtok"])
    gt = m["gt"]
    pg.op("dve", lambda e: e.tensor_tensor(out=gt, in0=pt[:, 768:784], in1=c.gateb[:], op=ALU.add), [ptk, "gateb"], ["gt"])
    gv = gt.rearrange("p (a b) -> p a b", a=2)[:, :, 4:8]
    pg.op("act", lambda e: e.activation(out=m["ge"], in_=gv, func=AF.Exp, scale=-1.0), ["gt"], ["ge"])
    pg.op("act", lambda e: e.activation(out=m["ge"], in_=m["ge"], func=AF.Ln, bias=1.0), ["ge"], ["ge"])
    pg.op("dve", lambda e: e.tensor_scalar(out=gv, in0=m["ge"], scalar1=-1.0, scalar2=None, op0=ALU.mult), ["ge", "gt"], ["gt"])
    pg.dma("sp", D_["G"][t * 128:(t + 1) * 128, :], gt, ["gt"], ["d:G"])
    pg.end_capture(head=1, group=1)


def mlstm_scan_gen(c, m, D_, d, need_ctx):
    pg = c.pg
    sb = m["scan"][d]
    order = [32, 33] + list(range(32)) if d == 0 else [33, 32] + list(range(31, -1, -1))
    X, Y = c.banks[4 + 2 * d], c.banks[5 + 2 * d]
    Xk, Yk = f"bank{4 + 2 * d}", f"bank{5 + 2 * d}"
    K = lambda n: f"{n}_{d}"
    C, Cb = sb["C"], sb["Cb"]
    pg.op("dve", lambda e: e.memset(C, 0.0), [], [K("C")])
    pg.op("dve", lambda e: e.memset(Cb, 0.0), [], [K("Cb")])
    tri_f = c.triu_f if d == 0 else c.tril_f
    tri_b = c.triu_b if d == 0 else c.tril_b
    trik = ("triu_f", "triu_b") if d == 0 else ("tril_f", "tril_b")
    HD = D_["HF"] if d == 0 else D_["HB"]

    def loads(n_):
        t = order[n_]
        i = n_ % 2
        rows = slice(t * 128, (t + 1) * 128)
        pg.dma("sp", sb["mTi"][i], D_["MQKT"][:, :, rows].rearrange("j d t -> d j t"), ["d:MQKT"], [K(f"mTi{i}")])
        pg.dma("sp", sb["ktok"][i], D_["MKtok"][rows, :], ["d:MKtok"], [K(f"ktok{i}")])
        pg.dma("sp", sb["gti"][i], D_["G"][rows, :], ["d:G"], [K(f"gti{i}")])
        pg.dma("sp", sb["v"][i], D_["ps"][rows, 256:512], [f"d:ps{t}"], [K(f"v{i}")])
    loads(0)
    yield
    for n_, t in enumerate(order):
        i = n_ % 2
        isctx = t >= NLT
        want_out = (not isctx) or need_ctx
        mTi, ktok, gti, v = sb["mTi"][i], sb["ktok"][i], sb["gti"][i], sb["v"][i]
        rows = slice(t * 128, (t + 1) * 128)
        if n_ + 1 < len(order):
            loads(n_ + 1)
        ig = gti[:, 8 * d:8 * d + 4]
        lf = gti[:, 8 * d + 4:8 * d + 8]
        pg.op("pe", lambda e, lf=lf: e.matmul(Y[:, 0:4], lhsT=tri_f[:, :], rhs=lf, start=True, stop=True), [K(f"gti{i}"), trik[0]], [Yk])
        pg.op("pe", lambda e, lf=lf: e.matmul(Y[:, 4:8], lhsT=c.ones_f[:, :], rhs=lf, start=True, stop=True), [K(f"gti{i}"), "ones_f"], [Yk])
        yield
        yield
        pg.op("act", lambda e: e.activation(out=sb["rF"], in_=Y[:, 0:8], func=AF.Exp), [Yk], [K("rF")])
        pg.op("dve", lambda e, ig=ig: e.tensor_tensor(out=sb["du"], in0=ig, in1=Y[:, 0:4], op=ALU.subtract), [K(f"gti{i}"), Yk], [K("du")])
        yield
        yield
        pg.op("act", lambda e: e.activation(out=sb["u"], in_=sb["du"], func=AF.Exp), [K("du")], [K("u")])
        for h in range(4):
            pg.op("pe", lambda e, h=h, mTi=mTi: e.matmul(X[:, h * 128:(h + 1) * 128], lhsT=mTi[:, 4 + h, :], rhs=mTi[:, h, :],
                                                         start=True, stop=True), [K(f"mTi{i}")], [Xk])
        yield
        yield
        pg.op("dve", lambda e, v=v: e.tensor_tensor(out=sb["vpp"][:, :, 0:64], in0=v.rearrange("p (h j) -> p h j", h=4),
                                                    in1=sb["u"].unsqueeze(2).broadcast_to([128, 4, 64]), op=ALU.mult),
              [K(f"v{i}"), K("u")], [K("vppA")])
        pg.op("dve", lambda e: e.tensor_copy(out=sb["vpp"][:, :, 64:65], in_=sb["u"].unsqueeze(2)), [K("u")], [K("vppB")])
        pg.op("dve", lambda e: e.tensor_tensor(out=sb["PTm"], in0=X[:, :].rearrange("p (h t) -> p h t", h=4),
                                               in1=tri_b[:, None, :].broadcast_to([128, 4, 128]), op=ALU.mult),
              [Xk, trik[1]], [K("PTm")])
        yield
        yield
        for h in range(4):
            pg.op("pe", lambda e, h=h: e.matmul(Y[:, 8 + h * 65:8 + (h + 1) * 65], lhsT=sb["PTm"][:, h, :], rhs=sb["vpp"][:, h, :],
                                                start=True, stop=False), [K("PTm"), K("vppA"), K("vppB")], [Yk])
            pg.op("pe", lambda e, h=h, mTi=mTi: e.matmul(Y[:, 8 + h * 65:8 + (h + 1) * 65], lhsT=mTi[:, h, :], rhs=Cb[:, h, :],
                                                         start=False, stop=True), [K(f"mTi{i}"), K("Cb")], [Yk])
        for h in range(4):
            pg.op("pe", lambda e, h=h, ktok=ktok: e.matmul(X[0:32, h * 65:(h + 1) * 65], lhsT=ktok[:, h * 32:(h + 1) * 32],
                                                           rhs=sb["vpp"][:, h, :], start=True, stop=True),
                  [K(f"ktok{i}"), K("vppA"), K("vppB")], [Xk])
        yield
        yield
        pg.op("dve", lambda e: e.tensor_tensor(out=C, in0=C, in1=X[0:32, 0:260].rearrange("p (h j) -> p h j", h=4), op=ALU.add),
              [K("C"), Xk], [K("C")])
        pg.op("dve", lambda e: e.tensor_tensor(out=C, in0=C, in1=sb["rF"][0:32, 4:8].unsqueeze(2).broadcast_to([32, 4, 65]), op=ALU.mult),
              [K("C"), K("rF")], [K("C")])
        pg.op("dve", lambda e: e.tensor_copy(out=Cb, in_=C), [K("C")], [K("Cb")])
        if not want_out:
            yield
            continue
        pg.op("dve", lambda e: e.tensor_tensor(out=sb["t1"], in0=Y[:, 8:268].rearrange("p (h j) -> p h j", h=4),
                                               in1=sb["rF"][:, 0:4].unsqueeze(2).broadcast_to([128, 4, 65]), op=ALU.mult),
              [Yk, K("rF")], [K("t1")])
        yield
        pg.op("dve", lambda e: e.tensor_scalar(out=sb["dneg"], in0=sb["t1"][:, :, 64], scalar1=-1.0, scalar2=None, op0=ALU.mult),
              [K("t1")], [K("dneg")])
        pg.op("dve", lambda e: e.tensor_tensor(out=sb["den"], in0=sb["t1"][:, :, 64], in1=sb["dneg"], op=ALU.max),
              [K("t1"), K("dneg")], [K("den")])
        pg.op("dve", lambda e: e.tensor_single_scalar(out=sb["den"], in_=sb["den"], scalar=1.0, op=ALU.max), [K("den")], [K("den")])
        pg.op("dve", lambda e: e.reciprocal(out=sb["den"], in_=sb["den"]), [K("den")], [K("den")])
        yield
        pg.op("dve", lambda e: e.tensor_tensor(out=sb["hd"], in0=sb["t1"][:, :, 0:64],
                                               in1=sb["den"].unsqueeze(2).broadcast_to([128, 4, 64]), op=ALU.mult),
              [K("t1"), K("den")], [K("hd")])
        pg.dma("sp", HD[rows, :], sb["hd"].rearrange("p h j -> p (h j)"), [K("hd")], ["d:HF" if d == 0 else "d:HB"])
        yield


def mlstm_combine(c, m, D_, need_ctx):
    pg = c.pg
    tiles = list(range(NTT)) if need_ctx else list(range(NLT))

    def loads(n_):
        t = tiles[n_]
        i = n_ % 2
        rows = slice(t * 128, (t + 1) * 128)
        pg.dma("sp", m["hfi"][i], D_["HF"][rows, :], ["d:HF"], [f"hfi{i}"])
        pg.dma("sp", m["hbi"][i], D_["HB"][rows, :], ["d:HB"], [f"hbi{i}"])
        pg.dma("sp", m["oi"][i], D_["ps"][rows, 512:768], [f"d:ps{t}"], [f"oi{i}"])
    pg.op("sp", None, [], ["pt0"])
    loads(0)
    for n_, t in enumerate(tiles):
        i = n_ % 2
        rows = slice(t * 128, (t + 1) * 128)
        if n_ + 1 < len(tiles):
            loads(n_ + 1)
        pg.op("dve", lambda e, i=i: e.tensor_tensor(out=m["hs"].rearrange("p h j -> p (h j)"), in0=m["hfi"][i], in1=m["hbi"][i], op=ALU.add),
              [f"hfi{i}", f"hbi{i}"], ["hs"])
        rstd_multi(c, m, m["hs"], 4, 64, m["sq"][:, 0:256].rearrange("p (h j) -> p h j", h=4), m["s4"], ["hs"], "s4h")
        pg.op("dve", lambda e: e.tensor_tensor(out=m["hq"], in0=m["hs"], in1=m["s4"][:, 0:4].unsqueeze(2).broadcast_to([128, 4, 64]),
                                               op=ALU.mult), ["hs", "s4h", "sqtmp"], ["hq"])
        pg.op("dve", lambda e: e.tensor_tensor(out=m["hq"], in0=m["hq"], in1=c.g_on[:].rearrange("p (h j) -> p h j", h=4),
                                               op=ALU.mult), ["hq", "g_on"], ["hq"])
        pg.op("act", lambda e, i=i: e.activation(out=m["sgo"], in_=m["oi"][i], func=AF.Exp, scale=-1.0), [f"oi{i}"], ["sgo"])
        pg.op("act", lambda e: e.activation(out=m["sgo"], in_=m["sgo"], func=AF.Ln, bias=1.0), ["sgo"], ["sgo"])
        pg.op("act", lambda e: e.activation(out=m["sgo"], in_=m["sgo"], func=AF.Exp, scale=-1.0), ["sgo"], ["sgo"])
        pg.op("dve", lambda e: e.tensor_tensor(out=m["ya"], in0=m["hq"].rearrange("p h j -> p (h j)"), in1=m["sgo"], op=ALU.mult),
              ["hq", "sgo"], ["ya"])
        TB_ = c.banks[7][:, :].bitcast(BF16)
        for kk in range(2):
            pg.op("pe", lambda e, kk=kk: e.transpose(TB_[:, kk * 128:(kk + 1) * 128], m["ya"][:, kk * 128:(kk + 1) * 128], c.ident_b[:, :]),
                  ["ya", "ident_b"], ["bank7"])
        pg.op("act", lambda e: e.activation(out=m["yaT"], in_=TB_[:, 0:256].rearrange("p (k t) -> p k t", k=2), func=AF.Copy),
              ["bank7"], ["yaT"])
        pg.dma("sp", D_["YT"][0:256, rows].rearrange("(k p) t -> p k t", p=128), m["yaT"], ["yaT"], ["d:YT"])


def run_tasks(tasks):
    tasks = list(tasks)
    while tasks:
        for g in list(tasks):
            try:
                next(g)
            except StopIteration:
                tasks.remove(g)


def attn_fin(c, m, bo, bd, bko, bkd, nq, ai, esink_ap=None, nh=1):
    pg = c.pg
    rd = m["rden"][ai]
    rk = f"rden{ai}"
    if esink_ap is not None:
        pg.op("dve", lambda e: e.tensor_tensor(out=rd[:, 0:nq].rearrange("p (h t) -> p h t", h=nh),
                                               in0=bd[0:64, 0:nq].rearrange("p (h t) -> p h t", h=nh),
                                               in1=esink_ap.unsqueeze(2).broadcast_to([64, nh, nq // nh]), op=ALU.add),
              [bkd, "esink64"], [rk])
        pg.op("dve", lambda e: e.reciprocal(out=rd[:, 0:nq], in_=rd[:, 0:nq]), [rk], [rk])
    else:
        pg.op("dve", lambda e: e.reciprocal(out=rd[:, 0:nq], in_=bd[0:64, 0:nq]), [bkd], [rk])
    yh = m["yh"][ai]
    pg.op("dve", lambda e: e.tensor_tensor(out=yh[:, 0:nq], in0=bo[0:64, 0:nq], in1=rd[:, 0:nq], op=ALU.mult),
          [bko, rk], [f"yh{ai}"])
    return yh


def run_units(units, look=2):
    n = len(units)
    for u in range(n + look):
        if u < n:
            units[u][0]()
        if u - look >= 0:
            units[u - look][1]()


def run_units_gen(units, look=1):
    n = len(units)
    for u in range(n + look):
        if u < n:
            units[u][0]()
        if u - look >= 0:
            units[u - look][1]()
        yield


def mla_attention_gen(c, m, D_, need_ctx, wide=True):
    pg = c.pg
    units = []
    cnt = 0
    qcnt = 0
    for h in range(6):
        KTh, QTh = m["KTh"][0], m["QTh"][0]
        blocks = [(qb * 512, 512, list(range(NTT))) for qb in range(8)]
        if need_ctx:
            blocks.append((4096, 256, [32, 33]))
        first = True
        for (q0, nq, kts) in blocks:
            ai = qcnt % 2
            qcnt += 1
            ob = 4 + ai
            for n_, kt in enumerate(kts):
                r = cnt % 4
                cnt += 1

                def stage_a(h=h, q0=q0, nq=nq, kt=kt, r=r, ld=first):
                    if ld:
                        pg.dma("sp", KTh, D_["KT"][h], ["d:KT"], ["KTh"])
                        pg.dma("sp", QTh, D_["QT"][h], ["d:QT"], ["QTh"])
                    bs = c.banks[r]
                    pg.op("pe", lambda e: e.matmul(bs[:, 0:nq], lhsT=KTh[:, kt * 128:(kt + 1) * 128], rhs=QTh[:, q0:q0 + nq],
                                                   start=True, stop=True), ["KTh", "QTh"], [f"bank{r}"])
                    PT = m["PT"][r]
                    pg.op("act", lambda e: e.activation(out=PT[:, 0:nq], in_=bs[:, 0:nq], func=AF.Exp, scale=B_SCALE),
                          [f"bank{r}"], [f"PT{r}"])

                def stage_b(h=h, q0=q0, nq=nq, kt=kt, r=r, n_=n_, L=len(kts), ob=ob, ai=ai):
                    bo = c.banks[ob]
                    PT = m["PT"][r]
                    pg.op("pe", lambda e: e.matmul(bo[0:65, 0:nq], lhsT=m["Vr"][:, kt, h, :], rhs=PT[:, 0:nq],
                                                   start=(n_ == 0), stop=(n_ == L - 1)), [f"PT{r}", f"Vr{kt}", "Vr1"], [f"bank{ob}"])
                    if n_ == L - 1:
                        accs = m["accs"][ai]
                        ak = f"accs{ai}"
                        pg.op("act", lambda e: e.activation(out=accs[:, 0:nq], in_=bo[0:65, 0:nq], func=AF.Copy), [f"bank{ob}"], [ak])
                        pg.op("dve", lambda e: e.reciprocal(out=accs[64:65, 0:nq], in_=accs[64:65, 0:nq]), [ak], [ak])

                def stage_c(h=h, q0=q0, nq=nq, ai=ai):
                    accs = m["accs"][ai]
                    ak = f"accs{ai}"
                    bb = c.banks[6 + ai]
                    pg.op("pe", lambda e: e.matmul(bb[0:64, 0:nq], lhsT=c.ones_f[64:65, 0:64], rhs=accs[64:65, 0:nq],
                                                   start=True, stop=True), [ak, "ones_f"], [f"bank{6 + ai}"])
                    yh = m["yh"][ai]
                    pg.op("dve", lambda e: e.tensor_tensor(out=yh[:, 0:nq], in0=bb[0:64, 0:nq], in1=accs[0:64, 0:nq], op=ALU.mult),
                          [ak, f"bank{6 + ai}"], [f"yh{ai}"])
                    pg.dma("sp", D_["YT"][256 + h * 64:256 + (h + 1) * 64, q0:q0 + nq], yh[:, 0:nq], [f"yh{ai}"], ["d:YT"])
                units.append((stage_a, stage_b, stage_c if n_ == len(kts) - 1 else None))
                first = False
    n = len(units)
    LOOK, DEF = 2, 8
    for u in range(n + LOOK + DEF + 1):
        if u < n:
            units[u][0]()
        if 0 <= u - LOOK < n:
            units[u - LOOK][1]()
        if 0 <= u - LOOK - DEF < n and units[u - LOOK - DEF][2] is not None:
            units[u - LOOK - DEF][2]()
        yield


def gqa_attention(c, m, D_, need_ctx):
    pg = c.pg
    units = []
    cnt = 0
    qcnt = 0
    nblk = NTT if need_ctx else NLT
    for g in range(2):
        for ib in range(nblk):
            qi = qcnt % 4
            ai = qcnt % 2
            qcnt += 1
            q3 = m["q3"][qi]
            if ib >= NLT:
                kts = [32, 33]
            else:
                kts = [32, 33] + ([ib - 1] if ib > 0 else []) + [ib] + ([ib + 1] if ib < NLT - 1 else [])
            for n_, kt in enumerate(kts):
                r = cnt % 4
                cnt += 1

                def stage_a(g=g, ib=ib, kt=kt, r=r, n_=n_, q3=q3, qi=qi, bidx=qcnt - 1):
                    if n_ == 0:
                        todo = [bidx + 2] if bidx > 0 else [0, 1, 2]
                        for bb in todo:
                            if bb < 2 * nblk:
                                g2, ib2 = bb // nblk, bb % nblk
                                pg.dma("sp", m["q3"][bb % 4],
                                       D_["GQT"][3 * g2:3 * g2 + 3, :, ib2 * 128:(ib2 + 1) * 128].rearrange("h d t -> d h t"),
                                       ["d:GQT"], [f"q3{bb % 4}"])
                    bs = c.banks[r]
                    pg.op("pe", lambda e: e.matmul(bs[:, 0:384], lhsT=m["GKT"][:, g, kt * 128:(kt + 1) * 128],
                                                   rhs=q3.rearrange("p h t -> p (h t)"), start=True, stop=True),
                          [f"GKT{kt}", f"q3{qi}"], [f"bank{r}"])
                    PT = m["PT"][r]
                    pg.op("act", lambda e: e.activation(out=PT[:, 0:384], in_=bs[:, 0:384], func=AF.Exp, scale=C_SCALE),
                          [f"bank{r}"], [f"PT{r}"])
                    msk = None
                    if ib < NLT and kt == ib - 1:
                        msk, mk = c.tril_b, "tril_b"
                    if ib < NLT and kt == ib + 1:
                        msk, mk = c.triu_b, "triu_b"
                    if msk is not None:
                        pg.op("dve", lambda e: e.tensor_tensor(out=PT[:, 0:384].rearrange("p (h t) -> p h t", h=3),
                                                               in0=PT[:, 0:384].rearrange("p (h t) -> p h t", h=3),
                                                               in1=msk[:, None, :].broadcast_to([128, 3, 128]), op=ALU.mult),
                              [f"PT{r}", mk], [f"PT{r}"])

                def stage_b(g=g, ib=ib, kt=kt, r=r, n_=n_, L=len(kts), ai=ai):
                    bo, bd = c.banks[4 + ai], c.banks[6 + ai]
                    PT = m["PT"][r]
                    pg.op("pe", lambda e: e.matmul(bo[0:64, 0:384], lhsT=m["GV"][:, kt, g, 0:64], rhs=PT[:, 0:384],
                                                   start=(n_ == 0), stop=(n_ == L - 1)), [f"PT{r}", f"GV{kt}"], [f"bank{4 + ai}"])
                    pg.op("pe", lambda e: e.matmul(bd[:, 0:384], lhsT=c.ones_b[:, :], rhs=PT[:, 0:384],
                                                   start=(n_ == 0), stop=(n_ == L - 1)), [f"PT{r}", "ones_b"], [f"bank{6 + ai}"])
                    if n_ == L - 1:
                        yh = attn_fin(c, m, bo, bd, f"bank{4 + ai}", f"bank{6 + ai}", 384, ai,
                                      esink_ap=c.esink64[:, 3 * g:3 * g + 3], nh=3)
                        for j in range(3):
                            hh = 3 * g + j
                            pg.dma("sp", D_["YT"][640 + hh * 64:640 + (hh + 1) * 64, ib * 128:(ib + 1) * 128],
                                   yh[:, j * 128:(j + 1) * 128], [f"yh{ai}"], ["d:YT"])
                units.append((stage_a, stage_b))
    run_units(units)

WNAMES = ["ada_w", "ada_b", "norm_g", "ffn1_wi", "ffn1_wo", "ffn2_wi", "ffn2_wo", "w_in", "w_out",
          "mlstm_gate_b", "mlstm_out_norm", "mla_cq_norm", "mla_ckv_norm", "mla_w_uq", "mla_w_ukv",
          "mla_q_norm", "mla_k_norm", "gqa_q_norm", "gqa_k_norm", "gqa_sink"]
WSHAPES = {"ada_w": [2, 1024, 9216], "ada_b": [2, 9216], "norm_g": [2, 3, 1024], "ffn1_wi": [2, 1024, 5632],
           "ffn1_wo": [2, 2816, 1024], "ffn2_wi": [2, 1024, 5632], "ffn2_wo": [2, 2816, 1024], "w_in": [2, 1024, 1840],
           "w_out": [2, 1024, 1024], "mlstm_gate_b": [2, 16], "mlstm_out_norm": [2, 256], "mla_cq_norm": [2, 256],
           "mla_ckv_norm": [2, 128], "mla_w_uq": [2, 256, 576], "mla_w_ukv": [2, 128, 768], "mla_q_norm": [2, 96],
           "mla_k_norm": [2, 96], "gqa_q_norm": [2, 64], "gqa_k_norm": [2, 64], "gqa_sink": [2, 6]}


def make_consts():
    ident = np.eye(128, dtype=np.float32)
    triu = np.triu(np.ones((128, 128), np.float32))
    tril = np.tril(np.ones((128, 128), np.float32))
    sel = np.zeros((2, 2, 128), np.float32)
    sel[0, 0, :] = 1
    sel[1, 1, :] = 1
    bf = ml_dtypes.bfloat16
    cst = {"ident_b": ident.astype(bf), "ident_f": ident, "triu_f": triu, "tril_f": tril,
           "ones_f": np.ones((128, 128), np.float32), "triu_b": triu.astype(bf), "tril_b": tril.astype(bf), "sel": sel}
    T = 4096
    pos = np.arange(T)
    row = (pos // 64).astype(np.float32)
    col = (pos % 64).astype(np.float32)

    def tab(half):
        inv = (10000.0 ** (-np.arange(half, dtype=np.float32) / half)).astype(np.float32)
        ar_, ac_ = row[:, None] * inv, col[:, None] * inv
        cr, sr, cc, sc = np.cos(ar_), np.sin(ar_), np.cos(ac_), np.sin(ac_)
        return np.concatenate([cr, cr, cc, cc], 1), np.concatenate([-sr, sr, -sc, sc], 1)
    c32, s32 = tab(8)
    c64, s64 = tab(16)
    cst["rope"] = np.concatenate([c32, s32, c64, s64], 1).astype(np.float32)
    return cst


CSHAPES = {"ident_b": ([128, 128], "bf"), "ident_f": ([128, 128], "f"), "triu_f": ([128, 128], "f"), "tril_f": ([128, 128], "f"),
           "ones_f": ([128, 128], "f"), "triu_b": ([128, 128], "bf"), "tril_b": ([128, 128], "bf"), "sel": ([2, 2, 128], "f"),
           "rope": ([4096, 192], "f")}


def blocks_of(tiles):
    return [tiles[i:i + TB] for i in range(0, len(tiles), TB)]


def build_program(debug=False):
    nc = bass.Bass("TRN2", target_bir_lowering=False)
    pg = Prog(nc)
    x_in = nc.dram_tensor("x_in", [4096, D], F32, kind="ExternalInput").ap()
    ctx_in = nc.dram_tensor("ctx_in", [256, D], F32, kind="ExternalInput").ap()
    c2 = nc.dram_tensor("c2", [2, D], F32, kind="ExternalInput").ap()
    W = {n: nc.dram_tensor(n, WSHAPES[n], F32, kind="ExternalInput").ap() for n in WNAMES}
    cst = {n: nc.dram_tensor("k_" + n, sh, F32 if ty == "f" else BF16, kind="ExternalInput").ap() for n, (sh, ty) in CSHAPES.items()}
    out = nc.dram_tensor("out", [4096, D], F32, kind="ExternalOutput").ap()
    kind = "ExternalOutput" if debug else "Internal"
    D_ = {}
    D_["xs"] = nc.dram_tensor("xs", [S, D], F32, kind=kind).ap()
    D_["ps"] = nc.dram_tensor("ps", [S, DIN], F32, kind=kind).ap()
    D_["KT"] = nc.dram_tensor("KT", [6, 96, S], BF16, kind="Internal").ap()
    D_["QT"] = nc.dram_tensor("QT", [6, 96, S], BF16, kind="Internal").ap()
    D_["GQT"] = nc.dram_tensor("GQT", [6, 64, S], BF16, kind="Internal").ap()
    D_["MQKT"] = nc.dram_tensor("MQKT", [8, 32, S], BF16, kind="Internal").ap()
    D_["MKtok"] = nc.dram_tensor("MKtok", [S, 128], BF16, kind="Internal").ap()
    D_["G"] = nc.dram_tensor("G", [S, 16], F32, kind="Internal").ap()
    D_["HF"] = nc.dram_tensor("HF", [S, 256], F32, kind="Internal").ap()
    D_["HB"] = nc.dram_tensor("HB", [S, 256], F32, kind="Internal").ap()
    D_["YT"] = nc.dram_tensor("YT", [D, S], BF16, kind=kind).ap()
    D_["rope"] = cst["rope"]
    WB = {}
    for l in range(DEPTH):
        for f in (1, 2):
            if (l, f) == (0, 1):
                continue
            WB[(l, f, "wi")] = nc.dram_tensor(f"wibf{l}{f}", [22 * 128, 2048], BF16, kind="Internal").ap()
            WB[(l, f, "wo")] = nc.dram_tensor(f"wobf{l}{f}", [22 * 128, 1024], BF16, kind="Internal").ap()

    def do_cast(l, f):
        cast_weights_wi(c, W[f"ffn{f}_wi"][l], WB[(l, f, "wi")], f"wi{l}{f}")
        cast_weights_wo(c, W[f"ffn{f}_wo"][l], WB[(l, f, "wo")], f"wo{l}{f}")

    c = setup_common(nc, pg)
    c.ar = Arena(nc, nc.sbuf_bytes_remaining - 2048)
    load_consts(c, cst)
    mod_setup(c, c2)

    def src_rows(l, t):
        if l == 0:
            return x_in[t * 128:(t + 1) * 128, :] if t < NLT else ctx_in[(t - NLT) * 128:(t - NLT + 1) * 128, :]
        return D_["xs"][t * 128:(t + 1) * 128, :]

    for l in range(DEPTH):
        need_ctx = l < DEPTH - 1
        ln = f"L{l}"
        pg.barrier()
        b = ffn_layout(c)
        mod_layer(c, W["ada_w"][l], W["ada_b"][l], W["norm_g"][l], b["wg"] + b["wu"],
                  [f"wg{i}" for i in range(3)] + [f"wu{i}" for i in range(3)], ln)
        for tiles in blocks_of(list(range(NTT))):
            nt = len(tiles)
            isctx = [t >= NLT for t in tiles]
            for j, t in enumerate(tiles):
                pg.dma("sp", b["x"][:, j, :], src_rows(l, t), [f"d:xs{t}"], [f"xb{j}"])
            if l == 0:
                ffn_block(c, b, list(range(nt)), isctx, 0, 0, W["ffn1_wi"][l], W["ffn1_wo"][l], ln + "a")
            else:
                ffn_block(c, b, list(range(nt)), isctx, 0, 0, WB[(l, 1, "wi")], WB[(l, 1, "wo")], f"{l}1", cached=True)
            win_block(c, b, tiles, isctx, W["w_in"][l], D_["ps"], ln)
            for j, t in enumerate(tiles):
                pg.dma("sp", D_["xs"][t * 128:(t + 1) * 128, :], b["x"][:, j, :], [f"xb{j}"], [f"d:xs{t}"])
        pg.barrier()
        m = mixer_layout(c)
        load_layer_small(c, m, W, l)
        do_cast(l, 2)
        if l + 1 < DEPTH:
            do_cast(l + 1, 1)
        for t in range(NTT):
            prep_tile(c, m, t, D_, need_ctx)
        import os
        if os.environ.get("MIXMODE", "2") == "1":
            run_tasks([mlstm_scan_gen(c, m, D_, 0, need_ctx)])
            run_tasks([mlstm_scan_gen(c, m, D_, 1, need_ctx)])
            run_tasks([mla_attention_gen(c, m, D_, need_ctx)])
        elif os.environ.get("MIXMODE", "2") == "2":
            run_tasks([mlstm_scan_gen(c, m, D_, 0, need_ctx), mlstm_scan_gen(c, m, D_, 1, need_ctx)])
            run_tasks([mla_attention_gen(c, m, D_, need_ctx)])
        else:
            run_tasks([mla_attention_gen(c, m, D_, need_ctx, wide=False), mlstm_scan_gen(c, m, D_, 0, need_ctx),
                       mlstm_scan_gen(c, m, D_, 1, need_ctx)])
        gqa_attention(c, m, D_, need_ctx)
        mlstm_combine(c, m, D_, need_ctx)
        pg.barrier()
        b = ffn_layout(c)
        pg.dma("pool", b["wout"], W["w_out"][l].rearrange("(k p) n -> p k n", p=128), ["d:wout" + ln], ["wout"])
        tl = list(range(NTT)) if need_ctx else list(range(NLT))
        for tiles in blocks_of(tl):
            nt = len(tiles)
            isctx = [t >= NLT for t in tiles]
            for j, t in enumerate(tiles):
                pg.dma("sp", b["x"][:, j, :], D_["xs"][t * 128:(t + 1) * 128, :], [f"d:xs{t}"], [f"xb{j}"])
            wout_block(c, b, tiles, isctx, D_["YT"], ln)
            ffn_block(c, b, list(range(nt)), isctx, 2, 2, WB[(l, 2, "wi")], WB[(l, 2, "wo")], f"{l}2", cached=True)
            for j, t in enumerate(tiles):
                if l == DEPTH - 1:
                    pg.dma("sp", out[t * 128:(t + 1) * 128, :], b["x"][:, j, :], [f"xb{j}"], ["d:out"])
                else:
                    pg.dma("sp", D_["xs"][t * 128:(t + 1) * 128, :], b["x"][:, j, :], [f"xb{j}"], [f"d:xs{t}"])
    pg.op("sp", None, ["d:out"], [])
    pg.barrier()
    st = pg.emit()
    return nc, st


_CACHE = {}


def kernel(**inputs):
    if "nc" not in _CACHE:
        _CACHE["nc"] = build_program()[0]
    nc = _CACHE["nc"]
    cst = make_consts()
    f32 = lambda a: np.ascontiguousarray(np.asarray(a, dtype=np.float32))
    x = f32(inputs["x"])
    ctx = f32(inputs["ctx"])
    cvec = f32(inputs["c"])
    cctx = f32(inputs["c_ctx"])
    in_maps = []
    for b_ in range(4):
        im = {"x_in": x[b_], "ctx_in": ctx[b_], "c2": np.stack([cvec[b_], cctx])}
        for n in WNAMES:
            im[n] = f32(inputs[n])
        for n, v in cst.items():
            im["k_" + n] = v
        in_maps.append(im)
    res = run_bass_kernel_spmd(nc, in_maps, core_ids=list(range(4)))
    return np.stack([res.results[b_]["out"] for b_ in range(4)]).astype(np.float32)
```
